# Optimizing a Trainium2 kernel written in Bass

```python
import jax
import jax.numpy as jnp
from jax import lax
import numpy as np

D_MODEL = 1024
BATCH = 8
SEQ = 4096
DEPTH = 2
DEC_BATCH = 32
DEC_SEQ = 32
PAST_LEN = 1024

CHUNK = 64
N_MIXERS = 2
N_ATTN_LAYERS = (DEPTH + 1) // 2
N_DN_LAYERS = DEPTH // 2
EPS = 1e-6
F32 = jnp.float32

WINDOW = 128
WIN_CHUNKS = WINDOW // CHUNK
ATTN_CACHE = WINDOW
N_HEADS = 16
N_KV_HEADS = 4
HEAD_DIM = 64
GQA_GROUP = N_HEADS // N_KV_HEADS
ATTN_WIDTH = N_HEADS * HEAD_DIM
KV_WIDTH = N_KV_HEADS * HEAD_DIM
ATTN_IN = 2 * ATTN_WIDTH + 2 * KV_WIDTH

DN_HEADS = 8
DN_KEY_DIM = 128
DN_VAL_DIM = 128
DN_QK_WIDTH = DN_HEADS * DN_KEY_DIM
DN_V_WIDTH = DN_HEADS * DN_VAL_DIM
DN_CONV_DIM = 2 * DN_QK_WIDTH + DN_V_WIDTH
CONV_WIDTH = 4
DN_IN = DN_CONV_DIM + DN_V_WIDTH + 2 * DN_HEADS

kernel_name = 'chunk_causal_swa_sink_gated_deltanet_hybrid_step'


def rmsnorm(x, g):
    xf = x.astype(F32)
    y = xf * lax.rsqrt(jnp.mean(xf * xf, axis=-1, keepdims=True) + EPS) * g.astype(F32)
    return y.astype(x.dtype)


def l2norm(x):
    return x * lax.rsqrt(jnp.sum(x * x, axis=-1, keepdims=True) + EPS)


def alibi_slopes():
    return 2.0 ** (-8.0 * jnp.arange(1, N_HEADS + 1, dtype=F32) / N_HEADS)


def sink_attention(q, k, v, q_pos, k_pos, k_valid, sinks):
    s = jnp.einsum('bnqkgd,bnskd->bnkgqs', q.astype(F32), k.astype(F32)) * (HEAD_DIM ** -0.5)
    slopes = alibi_slopes().reshape(N_KV_HEADS, GQA_GROUP, 1, 1)
    dist = jnp.abs(q_pos[:, :, None] - k_pos[:, None, :])
    s = s - slopes * dist[:, None, None]
    s = jnp.where(k_valid[:, None, None, None, :], s, -jnp.inf)
    sink = sinks.astype(F32).reshape(N_KV_HEADS, GQA_GROUP, 1, 1)
    m = jnp.maximum(jnp.max(s, axis=-1, keepdims=True), sink)
    p = jnp.exp(s - m)
    den = jnp.sum(p, axis=-1, keepdims=True) + jnp.exp(sink - m)
    return jnp.einsum('bnkgqs,bnskd->bnqkgd', p / den, v.astype(F32))


def attn_project(h, w_in):
    B, L, _ = h.shape
    q, k, v, gate = jnp.split(h @ w_in, [ATTN_WIDTH, ATTN_WIDTH + KV_WIDTH, ATTN_WIDTH + 2 * KV_WIDTH], axis=-1)
    return (q.reshape(B, L, N_KV_HEADS, GQA_GROUP, HEAD_DIM),
            k.reshape(B, L, N_KV_HEADS, HEAD_DIM),
            v.reshape(B, L, N_KV_HEADS, HEAD_DIM), gate)


def attn_prompt(h, w_in, sinks, w_out):
    B, L, _ = h.shape
    nc = L // CHUNK
    q, k, v, gate = attn_project(h, w_in)

    def band(t):
        tc = t.reshape(B, nc, CHUNK, N_KV_HEADS, HEAD_DIM)
        tp = jnp.pad(tc, ((0, 0), (WIN_CHUNKS, 0), (0, 0), (0, 0), (0, 0)))
        return jnp.concatenate([tp[:, j:j + nc] for j in range(WIN_CHUNKS + 1)], axis=2)

    qb = q.reshape(B, nc, CHUNK, N_KV_HEADS, GQA_GROUP, HEAD_DIM)
    q_pos = (jnp.arange(nc)[:, None] * CHUNK + jnp.arange(CHUNK)[None, :]).astype(F32)
    k_pos = (jnp.arange(nc)[:, None] - WIN_CHUNKS) * CHUNK + jnp.arange((WIN_CHUNKS + 1) * CHUNK)[None, :]
    o = sink_attention(qb, band(k), band(v), q_pos, k_pos.astype(F32), k_pos >= 0, sinks)
    o = o.reshape(B, L, ATTN_WIDTH).astype(h.dtype)
    y = (o * jax.nn.silu(gate)) @ w_out
    return y, k[:, L - ATTN_CACHE:], v[:, L - ATTN_CACHE:]


def attn_sample(h, cache_k, cache_v, w_in, sinks, w_out):
    B, T, _ = h.shape
    C = cache_k.shape[1]
    q, k, v, gate = attn_project(h, w_in)
    k_all = jnp.concatenate([cache_k.astype(k.dtype), k], axis=1)
    v_all = jnp.concatenate([cache_v.astype(v.dtype), v], axis=1)
    q_pos = (PAST_LEN + jnp.arange(T))[None, :].astype(F32)
    k_pos = (PAST_LEN - C + jnp.arange(C + T))[None, :].astype(F32)
    valid = jnp.ones((1, C + T), dtype=bool)
    o = sink_attention(q[:, None], k_all[:, None], v_all[:, None], q_pos, k_pos, valid, sinks)
    o = o.reshape(B, T, ATTN_WIDTH).astype(h.dtype)
    y = (o * jax.nn.silu(gate)) @ w_out
    return y, k_all[:, T:], v_all[:, T:]


def causal_conv(x, hist, w):
    L = x.shape[1]
    xp = jnp.concatenate([hist.astype(x.dtype), x], axis=1)
    y = sum(xp[:, j:j + L] * w[j] for j in range(CONV_WIDTH))
    return jax.nn.silu(y), xp[:, L:]


def gated_delta_chunked(q, k, v, g, beta, s0, chunk):
    B, L, H, dk = q.shape
    dv = v.shape[-1]
    n = L // chunk
    blk4 = lambda t: t.reshape(B, n, chunk, H, t.shape[-1]).transpose(0, 1, 3, 2, 4)
    q, k, v = blk4(q), blk4(k), blk4(v)
    g = g.reshape(B, n, chunk, H).transpose(0, 1, 3, 2)
    beta = beta.reshape(B, n, chunk, H).transpose(0, 1, 3, 2)
    gc = jnp.cumsum(g, axis=-1)
    tri = jnp.tril(jnp.ones((chunk, chunk), dtype=bool))
    strict = jnp.tril(jnp.ones((chunk, chunk), dtype=bool), -1)
    decay = jnp.exp(jnp.where(tri, gc[..., :, None] - gc[..., None, :], -jnp.inf))
    kb = k * beta[..., None]
    vb = v * beta[..., None]
    m = jnp.where(strict, jnp.einsum('bnhid,bnhjd->bnhij', kb, k) * decay, 0.0)
    a = m + jnp.eye(chunk, dtype=F32)
    rhs = jnp.concatenate([vb, kb * jnp.exp(gc)[..., None]], axis=-1)
    sol = lax.linalg.triangular_solve(a, rhs, left_side=True, lower=True, unit_diagonal=True)
    u, w = sol[..., :dv], sol[..., dv:]
    intra = jnp.where(tri, jnp.einsum('bnhid,bnhjd->bnhij', q, k) * decay, 0.0)

    def step(s, xs):
        qi, ki, ui, wi, gi, ai = xs
        v_new = ui - jnp.einsum('bhcd,bhde->bhce', wi, s)
        o = jnp.einsum('bhcd,bhde->bhce', qi * jnp.exp(gi)[..., None], s) + jnp.einsum('bhij,bhje->bhie', ai, v_new)
        g_last = gi[..., -1]
        s = s * jnp.exp(g_last)[..., None, None] + jnp.einsum(
            'bhcd,bhce->bhde', ki * jnp.exp(g_last[..., None] - gi)[..., None], v_new)
        return s, o

    xs = tuple(jnp.moveaxis(t, 1, 0) for t in (q, k, u, w, gc, intra))
    s_fin, o = lax.scan(step, s0, xs)
    o = o.transpose(1, 0, 3, 2, 4).reshape(B, L, H, dv)
    return o, s_fin


def deltanet_branch(h, conv_hist, s0, w_in, conv_w, a_log, dt_bias, norm_g, w_out):
    B, L, _ = h.shape
    qkv, z, b, a = jnp.split(h @ w_in, [DN_CONV_DIM, DN_CONV_DIM + DN_V_WIDTH, DN_CONV_DIM + DN_V_WIDTH + DN_HEADS], axis=-1)
    qkv, new_hist = causal_conv(qkv, conv_hist, conv_w)
    q, k, v = jnp.split(qkv.astype(F32), [DN_QK_WIDTH, 2 * DN_QK_WIDTH], axis=-1)
    q = l2norm(q.reshape(B, L, DN_HEADS, DN_KEY_DIM)) * (DN_KEY_DIM ** -0.5)
    k = l2norm(k.reshape(B, L, DN_HEADS, DN_KEY_DIM))
    v = v.reshape(B, L, DN_HEADS, DN_VAL_DIM)
    beta = jax.nn.sigmoid(b.astype(F32))
    g = -jnp.exp(a_log.astype(F32)) * jax.nn.softplus(a.astype(F32) + dt_bias.astype(F32))
    o, s_new = gated_delta_chunked(q, k, v, g, beta, s0.astype(F32), min(CHUNK, L))
    o = rmsnorm(o, norm_g) * jax.nn.silu(z.astype(F32).reshape(B, L, DN_HEADS, DN_VAL_DIM))
    o = o.reshape(B, L, DN_V_WIDTH).astype(h.dtype)
    return o @ w_out, new_hist, s_new.astype(s0.dtype)


def setup_inputs(seed: int = 0) -> dict:
    key = jax.random.key(seed)
    ks = jax.random.split(key, 20)
    nrm = lambda k, shape, scale: scale * jax.random.normal(k, shape, F32)
    dt = jnp.exp(jax.random.uniform(ks[15], (N_DN_LAYERS, DN_HEADS), F32, np.log(1e-3), np.log(1e-1)))
    return {
        'x_prompt': nrm(ks[0], (BATCH, SEQ, D_MODEL), 1.0),
        'x_sample': nrm(ks[1], (DEC_BATCH, DEC_SEQ, D_MODEL), 1.0),
        'cache_k': nrm(ks[2], (N_ATTN_LAYERS, DEC_BATCH, min(WINDOW, PAST_LEN), N_KV_HEADS, HEAD_DIM), 1.0),
        'cache_v': nrm(ks[3], (N_ATTN_LAYERS, DEC_BATCH, min(WINDOW, PAST_LEN), N_KV_HEADS, HEAD_DIM), 1.0),
        'state_conv': nrm(ks[4], (N_DN_LAYERS, DEC_BATCH, CONV_WIDTH - 1, DN_CONV_DIM), 1.0),
        'state_ssm': nrm(ks[5], (N_DN_LAYERS, DEC_BATCH, DN_HEADS, DN_KEY_DIM, DN_VAL_DIM), 0.1),
        'norm_g': 1.0 + nrm(ks[6], (DEPTH, D_MODEL), 0.02),
        'final_norm_g': 1.0 + nrm(ks[7], (D_MODEL,), 0.02),
        'attn_w_in': nrm(ks[8], (N_ATTN_LAYERS, D_MODEL, ATTN_IN), D_MODEL ** -0.5),
        'attn_sinks': nrm(ks[9], (N_ATTN_LAYERS, N_HEADS), 0.5),
        'attn_w_out': nrm(ks[10], (N_ATTN_LAYERS, ATTN_WIDTH, D_MODEL), ATTN_WIDTH ** -0.5),
        'dn_w_in': nrm(ks[11], (N_DN_LAYERS, D_MODEL, DN_IN), D_MODEL ** -0.5),
        'dn_conv_w': nrm(ks[12], (N_DN_LAYERS, CONV_WIDTH, DN_CONV_DIM), CONV_WIDTH ** -0.5),
        'dn_a_log': jnp.log(jax.random.uniform(ks[13], (N_DN_LAYERS, DN_HEADS), F32, 1.0, 16.0)),
        'dn_dt_bias': dt + jnp.log(-jnp.expm1(-dt)),
        'dn_norm_g': 1.0 + nrm(ks[14], (N_DN_LAYERS, DN_VAL_DIM), 0.02),
        'dn_w_out': nrm(ks[16], (N_DN_LAYERS, DN_V_WIDTH, D_MODEL), DN_V_WIDTH ** -0.5),
    }


def reference(x_prompt, x_sample, cache_k, cache_v, state_conv, state_ssm, norm_g, final_norm_g,
              attn_w_in, attn_sinks, attn_w_out, dn_w_in, dn_conv_w, dn_a_log, dn_dt_bias,
              dn_norm_g, dn_w_out):
    xp, xs = x_prompt, x_sample
    kp_l, vp_l, cp_l, sp_l = [], [], [], []
    ks_l, vs_l, cs_l, ss_l = [], [], [], []
    for i in range(DEPTH):
        hp = rmsnorm(xp, norm_g[i])
        hs = rmsnorm(xs, norm_g[i])
        j = i // N_MIXERS
        if i % N_MIXERS == 0:
            yp, kp, vp = attn_prompt(hp, attn_w_in[j], attn_sinks[j], attn_w_out[j])
            ys, kd, vd = attn_sample(hs, cache_k[j], cache_v[j], attn_w_in[j], attn_sinks[j], attn_w_out[j])
            kp_l.append(kp); vp_l.append(vp); ks_l.append(kd); vs_l.append(vd)
        else:
            hist0 = jnp.zeros((hp.shape[0], CONV_WIDTH - 1, DN_CONV_DIM), hp.dtype)
            s00 = jnp.zeros((hp.shape[0], DN_HEADS, DN_KEY_DIM, DN_VAL_DIM), state_ssm.dtype)
            yp, cp, sp = deltanet_branch(hp, hist0, s00, dn_w_in[j], dn_conv_w[j], dn_a_log[j],
                                         dn_dt_bias[j], dn_norm_g[j], dn_w_out[j])
            ys, cd, sd = deltanet_branch(hs, state_conv[j], state_ssm[j], dn_w_in[j], dn_conv_w[j],
                                         dn_a_log[j], dn_dt_bias[j], dn_norm_g[j], dn_w_out[j])
            cp_l.append(cp); sp_l.append(sp); cs_l.append(cd); ss_l.append(sd)
        xp = xp + yp
        xs = xs + ys
    y_prompt = rmsnorm(xp, final_norm_g)
    y_sample = rmsnorm(xs, final_norm_g)
    return (y_prompt, y_sample,
            jnp.stack(kp_l), jnp.stack(vp_l), jnp.stack(cp_l), jnp.stack(sp_l),
            jnp.stack(ks_l), jnp.stack(vs_l), jnp.stack(cs_l), jnp.stack(ss_l))
```

```python
import numpy as np
import os as _os_
from contextlib import ExitStack
import concourse.bass as bass
import concourse.mybir as mybir
from concourse.bass_utils import run_bass_kernel_spmd

F32 = mybir.dt.float32
BF16 = mybir.dt.bfloat16
AF = mybir.ActivationFunctionType
ALU = mybir.AluOpType

NCORES = 8
D = 1024
EPS = 1e-6
NT_FULL = 32
SAME_ENG_SYNC = (_os_.environ.get('K_SES', '1') == '1')
import os as _os
STOP = int(_os.environ.get('K_STOP', '0'))
ASTOP = int(_os.environ.get('K_ASTOP', '99'))
GSTAG = int(_os.environ.get('K_GSTAG', '12'))
OVERLAP = (_os.environ.get('K_OVERLAP', '1') == '1')
OVERLAP0 = (_os.environ.get('K_OVERLAP0', '1') == '1')
BSTEPS = int(_os.environ.get('K_BSTEPS', '3'))
AUXSTEPS = int(_os.environ.get('K_AUXSTEPS', '1'))
PLAN = (_os.environ.get('K_PLAN', '1') == '1')
PESTEP = int(_os.environ.get('K_PESTEP', '8'))
SKIP = set(_os.environ.get('K_SKIP', '').split(','))


class Sched:
    def __init__(self, nc, es):
        self.nc = nc
        self.es = es
        self.eng = {"pe": nc.tensor, "act": nc.scalar, "dve": nc.vector, "pool": nc.gpsimd, "sp": nc.sync}
        self.sem = {}
        self.cnt = {}
        for e in ["pe", "act", "dve", "pool"]:
            self.sem[e] = es.enter_context(nc.semaphore("s_" + e))
            self.cnt[e] = 0
        self.seen = {e: {} for e in self.eng}
        self.lastw = {}
        self.readers = {}
        self.pending = {e: 0 for e in self.eng}
        self.rec = None

    def dma_sem(self, key):
        if key not in self.sem:
            self.sem[key] = self.es.enter_context(self.nc.semaphore("d_" + key))
            self.cnt[key] = 0
        return key

    def _wait(self, eng, key, val):
        if val <= 0 or self.seen[eng].get(key, 0) >= val:
            return
        self.eng[eng].wait_ge(self.sem[key], val)
        self.seen[eng][key] = val

    def op(self, eng, reads, writes, fn, inc=True, dsem=None):
        if self.rec is not None:
            self.rec.append((eng, list(reads), list(writes), fn, inc, dsem))
            return None
        writes = list(writes) + [r for r in reads if r.startswith("ps") and r not in writes]
        deps = {}

        def add(tok):
            k, v = tok
            if deps.get(k, 0) < v:
                deps[k] = v

        for r in reads:
            if r in self.lastw:
                add(self.lastw[r])
        for w in writes:
            if w in self.lastw:
                add(self.lastw[w])
            for k, v in self.readers.get(w, {}).items():
                add((k, v))
        if dsem is not None:
            self.dma_sem(dsem)
            add((dsem, self.cnt[dsem]))
        for k, v in deps.items():
            if k == eng and (eng == "pe" or not SAME_ENG_SYNC):
                continue
            self._wait(eng, k, v)
        inst = fn()
        if dsem is not None:
            self.cnt[dsem] += 16
            inst.then_inc(self.sem[dsem], 16)
            tok = (dsem, self.cnt[dsem])
        elif inc:
            self.cnt[eng] += 1
            inst.then_inc(self.sem[eng], 1)
            tok = (eng, self.cnt[eng])
        else:
            tok = (eng, self.cnt[eng] + 1)
        for w in writes:
            self.lastw[w] = tok
            self.readers[w] = {}
        for r in reads:
            d = self.readers.setdefault(r, {})
            if d.get(tok[0], 0) < tok[1]:
                d[tok[0]] = tok[1]
        return tok

    def barrier(self):
        for e in self.eng:
            for k in list(self.sem.keys()):
                if k == e and e == "pe":
                    continue
                self._wait(e, k, self.cnt[k])

    def finish(self):
        for k in list(self.sem.keys()):
            self._wait("sp", k, self.cnt[k])


NEG = -30000.0


_SMALL = ("sc_", "ss", "ms", "rstd", "sso", "r16", "rec_s")


def _op_cost(op):
    eng, reads, writes = op[0], op[1], op[2]
    if eng == "pe":
        big = any(r in ("wz", "wo1", "wo", "wkv") for r in reads)
        return 0.26 if big else 0.09
    if eng == "sp":
        return 0.05
    small = any(w.startswith(_SMALL) for w in writes)
    if eng == "pool":
        return 0.9 if small else 1.15
    if small:
        return 0.15
    return 0.68


class Planner:
    def __init__(self):
        self.free = {}
        self.wr = {}
        self.rd = {}

    def est(self, step):
        eng = step[0][0]
        start = self.free.get(eng, 0.0)
        for op in step:
            for r in op[1]:
                start = max(start, self.wr.get(r, 0.0))
                if r.startswith("ps"):
                    start = max(start, self.rd.get(r, 0.0))
            for w in op[2]:
                start = max(start, self.wr.get(w, 0.0), self.rd.get(w, 0.0))
        return start

    def commit(self, step):
        eng = step[0][0]
        t = self.est(step)
        for op in step:
            t += _op_cost(op)
        self.free[eng] = t
        lat = 2.0 if eng == "sp" else 0.08
        for op in step:
            for w in op[2]:
                self.wr[w] = t + lat
            for r in op[1]:
                if self.rd.get(r, 0.0) < t + lat:
                    self.rd[r] = t + lat
        return t

    def merge(self, streams, emit):
        ptr = [0] * len(streams)
        while True:
            best, bs = None, None
            for i, st in enumerate(streams):
                while ptr[i] < len(st) and st[ptr[i]][0][0] == "gate" and ptr[st[ptr[i]][0][1]] >= st[ptr[i]][0][2]:
                    ptr[i] += 1
                if ptr[i] < len(st) and st[ptr[i]][0][0] != "gate":
                    e = self.est(st[ptr[i]])
                    if best is None or e < bs - 1e-9:
                        best, bs = i, e
            if best is None:
                assert all(ptr[i] >= len(st) for i, st in enumerate(streams)), "planner deadlock"
                break
            step = streams[best][ptr[best]]
            self.commit(step)
            emit(step)
            ptr[best] += 1


def _alibi_tables():
    slopes = 2.0 ** (-8.0 * np.arange(1, 17, dtype=np.float64) / 16.0)
    tabs = np.full((128, 8, 512), NEG, np.float32)
    q = np.arange(64)[None, :]
    s = np.arange(128)[:, None]
    for c in range(2):
        for blk in range(2):
            for half in range(2):
                ti = c * 4 + blk * 2 + half
                if c == 0 and blk == 0:
                    dist = q + 128 - s; valid = np.ones((128, 1), bool)
                elif c == 0 and blk == 1:
                    dist = np.abs(q - s); valid = s < 64
                elif c == 1 and blk == 0:
                    dist = np.abs(64 + q - s); valid = np.ones((128, 1), bool)
                else:
                    dist = 192 + q - s; valid = s >= 64
                for kv in range(4):
                    for j in range(2):
                        h = 4 * kv + 2 * j + half
                        v = np.where(valid, -slopes[h] * dist, NEG)
                        tabs[:, ti, (kv * 2 + j) * 64:(kv * 2 + j + 1) * 64] = v
    tabs2 = np.full((128, 10, 256), NEG, np.float32)
    t = np.arange(32)[None, :]
    for half in range(2):
        for kv in range(4):
            for j in range(2):
                h = 4 * kv + 2 * j + half
                cs = slice((kv * 2 + j) * 32, (kv * 2 + j + 1) * 32)
                tabs2[:, half, cs] = -slopes[h] * (t + 128 - s)
                for b in range(4):
                    sp = s - 32 * b
                    valid = (sp >= 0) & (sp < 32)
                    tabs2[:, 2 + 2 * b + half, cs] = np.where(valid, -slopes[h] * np.abs(t - sp), NEG)
    return tabs, tabs2


def _masks():
    p = np.arange(128)[:, None]
    f = np.arange(128)[None, :]
    same = (p // 32) == (f // 32)
    m = np.zeros((128, 10, 128), np.float32)
    m[:, 0] = (p <= f)
    m[:, 1] = (p > f)
    m[:, 2] = (f > p)
    m[:, 3] = (f >= p)
    m[:, 4] = 1.0
    m[:, 5] = (p <= f) & same
    m[:, 6] = (p > f) & same
    m[:, 7] = (f > p) & same
    m[:, 8] = (f >= p) & same
    m[:, 9] = same
    return m


def _masks_l1():
    m = _masks()
    p = np.arange(128)[:, None]
    f = np.arange(128)[None, :]
    same32 = (p // 32) == (f // 32)
    same64 = (p // 64) == (f // 64)
    off64 = (same64 & ~same32).astype(np.float32)
    off128 = (~same64).astype(np.float32)
    return np.ascontiguousarray(np.concatenate([m[:, [0, 1, 2, 3, 9]], off64[:, None, :], off128[:, None, :]], axis=1))


def build_program(NT, layers=2, debug=False):
    nc = bass.Bass("TRN2", target_bir_lowering=False)

    def din(name, shape):
        return nc.dram_tensor(name, list(shape), F32, kind="ExternalInput").ap()

    def dout(name, shape):
        return nc.dram_tensor(name, list(shape), F32, kind="ExternalOutput").ap()

    xp = din("xp", [NT * 128, D])
    xs = din("xs", [128, D])
    ck = din("ck", [4, 128, 256])
    cv = din("cv", [4, 128, 256])
    sconv = din("sconv", [128, 24, 4, 3])
    sssm = din("sssm", [4, 8, 128, 128])
    gcol = din("gcol", [128, 3, 8])
    fgb = din("fgb", [128, D])
    w_in0 = din("w_in0", [D, 2560])
    w_out0 = din("w_out0", [D, D])
    w_in1 = din("w_in1", [D, 4112])
    w_out1 = din("w_out1", [D, D])
    sinkrow = din("sinkrow", [1, 1536])
    cwl = din("cwl", [128, 96])
    alogb = din("alogb", [128, 8])
    dtbb = din("dtbb", [128, 8])
    dngb = din("dngb", [128, 128])
    identd = din("ident", [128, 128])
    tabsd = din("tabs", [128, 8, 512])
    tabs2d = din("tabs2", [128, 10, 256])
    masksd = din("masks", [128, 7, 128])

    yp = dout("yp", [NT * 128, D])
    ys = dout("ys", [128, D])
    kwp = dout("kwp", [128, 256])
    vwp = dout("vwp", [128, 256])
    convp = dout("convp", [3, 3072])
    ssmp = dout("ssmp", [8, 128, 128])
    kws = dout("kws", [4, 128, 256])
    vws = dout("vws", [4, 128, 256])
    convs = dout("convs", [4, 3, 3072])
    ssms = dout("ssms", [4, 8, 128, 128])
    if debug:
        x1d = dout("x1d", [(NT + 1) * 128, D])
    else:
        x1d = nc.dram_tensor("x1d", [(NT + 1) * 128, D], F32, kind="Internal").ap()

    with ExitStack() as es:
        S = Sched(nc, es)

        def sb(st, name, shape, dt):
            return st.enter_context(nc.sbuf_tensor("sb_" + name, list(shape), dt))

        ps = es.enter_context(nc.psum_tensor("ps", [128, 8, 512], F32))

        def psb(bank):
            return ps[:, bank, :].bitcast(BF16)

        ident = sb(es, "ident", [128, 128], F32)
        identb = sb(es, "identb", [128, 128], BF16)
        onesb = sb(es, "onesb", [128, 128], BF16)
        neghalf = sb(es, "neghalf", [128, 16], F32)
        gcs = sb(es, "gcs", [128, 3, 8], F32)
        S.op("sp", [], ["ident"], lambda: nc.sync.dma_start(out=ident[:], in_=identd), dsem="c")
        S.op("sp", [], ["gcs"], lambda: nc.sync.dma_start(out=gcs[:], in_=gcol), dsem="c")
        S.op("dve", ["ident"], ["identb"], lambda: nc.vector.tensor_copy(out=identb[:], in_=ident[:]))
        S.op("dve", [], ["onesb"], lambda: nc.vector.memset(onesb[:], 1.0))
        S.op("dve", [], ["neghalf"], lambda: nc.vector.memset(neghalf[:], -0.5))

        def load_weight(st_w, src, ncols, dst_fn, lidx, tag):
            srcv = src.rearrange("(k p) n -> p k n", p=128)
            nch = (ncols + 511) // 512
            for ci in range(nch):
                c0 = ci * 512
                c1 = min(ncols, c0 + 512)
                stg = st_w[ci % 2]
                nm = "stg%d" % (ci % 2)
                S.op("sp", [], [nm], lambda stg=stg, c0=c0, c1=c1: nc.sync.dma_start(
                    out=stg[:, :, 0:c1 - c0], in_=srcv[:, :, c0:c1]), dsem="w" + nm)
                dst_fn(c0, c1, stg, nm, "dve" if ci % 2 == 0 else "pool")

        def scaled_cast(eng, dst, stg, nm, w, lidx, dname):
            e = nc.vector if eng == "dve" else nc.gpsimd
            if lidx is None:
                S.op(eng, [nm], [dname], lambda: e.tensor_copy(out=dst, in_=stg[:, :, 0:w]))
            else:
                S.op(eng, [nm, "gcs"], [dname], lambda: e.tensor_tensor(
                    out=dst, in0=stg[:, :, 0:w],
                    in1=gcs[:, lidx, :].unsqueeze(2).broadcast_to([128, 8, w]), op=ALU.mult))

        def rms_and_transpose(xt, xtn, ssn, xn, xnT, st, jn="junk", emit=None):
            emit = emit or S.op
            junk, ss, ms, rstd = st
            emit("act", [xtn], (jn if isinstance(jn, list) else [jn]) + ["ss"], lambda: nc.scalar.activation(
                out=junk[:], in_=xt[:], func=AF.Square, accum_out=ss[:, 0:1]))
            emit("dve", ["ss"], ["ms"], lambda: nc.vector.tensor_scalar(
                out=ms[:, 0:1], in0=ss[:, 0:1], scalar1=1.0 / D, scalar2=EPS, op0=ALU.mult, op1=ALU.add))
            emit("pool", ["ms", "neghalf"], ["rstd"], lambda: nc.gpsimd.tensor_tensor(
                out=rstd[:, 0:1], in0=ms[:, 0:1], in1=neghalf[:, 0:1], op=ALU.pow))
            emit("dve", [xtn, "rstd"], ["xn"], lambda: nc.vector.tensor_scalar(
                out=xn[:], in0=xt[:], scalar1=rstd[:, 0:1], scalar2=None, op0=ALU.mult))
            pT = psb(0)
            for k in range(8):
                emit("pe", ["xn", "identb"], ["ps0"], lambda k=k: nc.tensor.transpose(
                    out=pT[:, k * 128:(k + 1) * 128], in_=xn[:, k * 128:(k + 1) * 128], identity=identb[:]),
                    inc=(k == 7))
            emit("act", ["ps0"], ["xnT"], lambda: nc.scalar.activation(
                out=xnT[:].rearrange("p k t -> p (k t)"), in_=pT[:, 0:1024], func=AF.Copy))

        with ExitStack() as e0:
            wq = sb(e0, "wq", [128, 8, 1024], BF16)
            wkd = sb(e0, "wkd", [128, 8, 512], BF16)
            wkv = sb(e0, "wkv", [128, 8, 512], BF16)
            wg = sb(e0, "wg", [128, 8, 1024], BF16)
            wo = sb(e0, "wo", [128, 8, 1024], BF16)
            tabs = sb(e0, "tabs", [128, 8, 512], F32)
            esink = sb(e0, "esink", [128, 1536], BF16)
            esf = sb(e0, "esf", [1, 1536], F32)
            tabs2 = sb(e0, "tabs2", [128, 10, 256], F32)
            S.op("sp", [], ["tabs"], lambda: nc.sync.dma_start(out=tabs2[:], in_=tabs2d), dsem="c")
            S.op("dve", [], ["esink"], lambda: nc.vector.memset(esink[:], 0.0))
            S.op("sp", [], ["tabs"], lambda: nc.sync.dma_start(out=tabs[:], in_=tabsd), dsem="c")
            S.op("sp", [], ["esf"], lambda: nc.sync.dma_start(out=esf[:], in_=sinkrow), dsem="c")
            S.op("act", ["esf"], ["esink"], lambda: nc.scalar.activation(out=esink[0:1, :], in_=esf[:], func=AF.Exp))

            with ExitStack() as ew:
                stg = [sb(ew, "stg0", [128, 8, 512], F32), sb(ew, "stg1", [128, 8, 512], F32)]

                def dst0(c0, c1, st, nm, eng):
                    if c0 < 1024:
                        scaled_cast(eng, wq[:, :, c0:c1], st, nm, 512, 0, "wq")
                    elif c0 == 1024:
                        scaled_cast(eng, wkv[:, :, :], st, nm, 512, 0, "wkv")
                        e = nc.vector if eng == "dve" else nc.gpsimd
                        wkd5 = wkd[:].rearrange("p k (v r d) -> p k v r d", v=4, r=2)
                        for r in range(2):
                            S.op(eng, [nm, "gcs"], ["wkd"], lambda r=r: e.tensor_tensor(
                                out=wkd5[:, :, :, r, :],
                                in0=st[:, :, 0:256].rearrange("p k (v d) -> p k v d", v=4),
                                in1=gcs[:, 0, :].unsqueeze(2).unsqueeze(3).broadcast_to([128, 8, 4, 64]),
                                op=ALU.mult))
                    else:
                        scaled_cast(eng, wg[:, :, c0 - 1536:c1 - 1536], st, nm, 512, 0, "wg")

                load_weight(stg, w_in0, 2560, dst0, 0, "wi0")
                load_weight(stg, w_out0, 1024,
                            lambda c0, c1, st, nm, eng: scaled_cast(eng, wo[:, :, c0:c1], st, nm, 512, None, "wo"),
                            None, "wo0")
                S.barrier()

            if layers >= 1:
                xts = [sb(e0, "xt%d" % i, [128, D], F32) for i in range(3)]
                x1t = [sb(e0, "x1t0", [128, D], F32), sb(e0, "x1t1", [128, D], F32)]
                junk = sb(e0, "junk", [128, D], BF16)
                ss = sb(e0, "ss", [128, 1], F32)
                ms = sb(e0, "ms", [128, 1], F32)
                rstd = sb(e0, "rstd", [128, 1], F32)
                xn = sb(e0, "xn", [128, D], BF16)
                xnT = sb(e0, "xnT", [128, 8, 128], BF16)
                qTs = [sb(e0, "qT%d" % i, [128, 8, 128], BF16) for i in range(2)]
                gTs = [sb(e0, "gT%d" % i, [128, 8, 128], BF16) for i in range(2)]
                ogT = sb(e0, "ogT", [128, 8, 128], BF16)
                kT = [sb(e0, "kT%d" % i, [128, 4, 128], BF16) for i in range(3)]
                vtok = [sb(e0, "vtok%d" % i, [128, 4, 64], BF16) for i in range(3)]
                kvout = sb(e0, "kvout", [128, 512], F32)
                tmp = [sb(e0, "tmp%d" % i, [128, 256], F32) for i in range(4)]
                pT_ = [sb(e0, "pT%d" % i, [128, 256], BF16) for i in range(8)]
                recs = [sb(e0, "rec%d" % i, [128, 256], F32) for i in range(2)]
                t1s = [sb(e0, "t1_%d" % i, [128, 256], F32) for i in range(2)]
                CS = [dict(A=2, B=3, OD=4, tmp=[0, 1], pT=[0, 1, 2, 3], ctr=[0], rec=recs[0], recn="rec0", t1=t1s[0], t1n="t1_0"),
                      dict(A=5, B=6, OD=7, tmp=[2, 3], pT=[4, 5, 6, 7], ctr=[0], rec=recs[1], recn="rec1", t1=t1s[1], t1n="t1_1")]
                ckf = [sb(e0, "ckf%d" % b, [128, 256], F32) for b in range(4)]
                cvf = [sb(e0, "cvf%d" % b, [128, 256], F32) for b in range(4)]
                ckdup = sb(e0, "ckdup", [128, 4, 128], BF16)
                kTc = [sb(e0, "kTc%d" % b, [128, 4, 128], BF16) for b in range(4)]
                cvb = [sb(e0, "cvb%d" % b, [128, 4, 64], BF16) for b in range(4)]
                tmpctr = [0]

                def attend(q0, QW, blocks, qT, qTn, gT, gTn, cs):
                    W2 = 4 * QW
                    nb = len(blocks)
                    OD = cs["OD"]
                    ODn = "ps%d" % OD
                    for kvg in range(2):
                        banks = (cs["A"], cs["B"])
                        pts = {}
                        for bi, blk in enumerate(blocks):
                            for kvi in range(2):
                                kv = 2 * kvg + kvi
                                for half in range(2):
                                    bank = banks[half]
                                    out = ps[:, bank, bi * W2 + kvi * 2 * QW: bi * W2 + (kvi + 1) * 2 * QW]
                                    S.op("pe", [blk["kn"], qTn], ["ps%d" % bank], lambda out=out, blk=blk, kv=kv, half=half, qT=qT: nc.tensor.matmul(
                                        out.rearrange("p (j q) -> p j q", j=2),
                                        lhsT=blk["kT"][half * 64:(half + 1) * 64, kv, :],
                                        rhs=qT[half * 64:(half + 1) * 64, 2 * kv:2 * kv + 2, q0:q0 + QW],
                                        start=True, stop=True), inc=(kvi == 1))
                            for half in range(2):
                                bank = banks[half]
                                tt, ti0 = blk["tab"]
                                ci = cs["ctr"][0]
                                cs["ctr"][0] += 1
                                tmi = cs["tmp"][ci % len(cs["tmp"])]
                                pti = cs["pT"][ci % len(cs["pT"])]
                                tm, tmn = tmp[tmi], "tmp%d" % tmi
                                pt, ptn = pT_[pti], "pT%d" % pti
                                tabv = tt[:, ti0 + half, kvg * W2:(kvg + 1) * W2]
                                S.op("dve", ["ps%d" % bank, "tabs"], [tmn], lambda tm=tm, bank=bank, bi=bi, tabv=tabv: nc.vector.tensor_tensor(
                                    out=tm[:, 0:W2], in0=ps[:, bank, bi * W2:(bi + 1) * W2], in1=tabv, op=ALU.add))
                                S.op("act", [tmn], [ptn], lambda tm=tm, pt=pt: nc.scalar.activation(
                                    out=pt[:, 0:W2], in_=tm[:, 0:W2], func=AF.Exp))
                                pts[(bi, half)] = (pt, ptn)
                        for kvi in range(2):
                            kv = 2 * kvg + kvi
                            for half in range(2):
                                o_out = ps[half * 64:(half + 1) * 64, OD, kvi * 2 * QW:(kvi + 1) * 2 * QW]
                                d_out = ps[half * 64:(half + 1) * 64, OD, 256 + kvi * 2 * QW:256 + (kvi + 1) * 2 * QW]
                                for bi, blk in enumerate(blocks):
                                    pt, ptn = pts[(bi, half)]
                                    S.op("pe", [ptn, blk["vn"]], [ODn], lambda o_out=o_out, blk=blk, kv=kv, pt=pt, bi=bi, kvi=kvi: nc.tensor.matmul(
                                        o_out, lhsT=blk["v"][:, kv, :], rhs=pt[:, kvi * 2 * QW:(kvi + 1) * 2 * QW],
                                        start=(bi == 0), stop=(bi == nb - 1)), inc=False)
                                for bi, blk in enumerate(blocks):
                                    pt, ptn = pts[(bi, half)]
                                    S.op("pe", [ptn, "onesb"], [ODn], lambda d_out=d_out, pt=pt, bi=bi, kvi=kvi: nc.tensor.matmul(
                                        d_out, lhsT=onesb[:, 0:64], rhs=pt[:, kvi * 2 * QW:(kvi + 1) * 2 * QW],
                                        start=(bi == 0), stop=False), inc=False)
                                e0c = (0 if QW == 64 else 1024) + half * 8 * QW + kv * 2 * QW
                                S.op("pe", ["esink", "onesb"], [ODn], lambda d_out=d_out, e0c=e0c: nc.tensor.matmul(
                                    d_out, lhsT=onesb[:, 0:64], rhs=esink[:, e0c:e0c + 2 * QW],
                                    start=False, stop=True), inc=True)
                        rec, recn, t1, t1n = cs["rec"], cs["recn"], cs["t1"], cs["t1n"]
                        S.op("dve", [ODn], [recn], lambda rec=rec: nc.vector.reciprocal(out=rec[:, 0:W2], in_=ps[:, OD, 256:256 + W2]))
                        S.op("dve", [ODn, recn], [t1n], lambda rec=rec, t1=t1: nc.vector.tensor_tensor(
                            out=t1[:, 0:W2], in0=ps[:, OD, 0:W2], in1=rec[:, 0:W2], op=ALU.mult))
                        S.op("pool", [t1n, gTn], ["ogT%d_%d" % (kvg, q0)], lambda t1=t1, gT=gT, kvg=kvg: nc.gpsimd.tensor_tensor(
                            out=ogT[:, 4 * kvg:4 * kvg + 4, q0:q0 + QW], in0=t1[:, 0:W2].rearrange("p (a q) -> p a q", a=4),
                            in1=gT[:, 4 * kvg:4 * kvg + 4, q0:q0 + QW], op=ALU.mult))

                def l0_load(t):
                    if t > NT:
                        return
                    is_s = (t == NT)
                    p3 = t % 3
                    xt = xts[p3]
                    xtn = "xt%d" % p3
                    src = xs if is_s else xp[t * 128:(t + 1) * 128, :]
                    S.op("sp", [], [xtn], lambda xt=xt, src=src: nc.sync.dma_start(out=xt[:], in_=src), dsem="ld" + xtn)
                    if is_s:
                        for b in range(4):
                            S.op("sp", [], ["ckf%d" % b], lambda b=b: nc.sync.dma_start(out=ckf[b][:], in_=ck[b]), dsem="cache")
                            S.op("sp", [], ["cvf%d" % b], lambda b=b: nc.sync.dma_start(out=cvf[b][:], in_=cv[b]), dsem="cache")

                def l0_front(t):
                    is_s = (t == NT)
                    par = t % 2
                    p3 = t % 3
                    pv3 = (t - 1) % 3
                    xt = xts[p3]
                    xtn = "xt%d" % p3
                    qT, qTn = qTs[par], "qT%d" % par
                    gT, gTn = gTs[par], "gT%d" % par
                    kTn = "kT%d" % p3
                    vn = "vtok%d" % p3
                    rms_and_transpose(xt, xtn, "ss", xn, xnT, (junk, ss, ms, rstd))
                    if STOP == 1:
                        return
                    for (wsrc, wn, dst, dn, func, scale) in ((wq, "wq", qT, qTn, AF.Copy, 0.125), (wg, "wg", gT, gTn, AF.Silu, 1.0)):
                        for hb in range(2):
                            bank = 1 - hb
                            for bi in range(4):
                                blk = hb * 4 + bi
                                for k in range(8):
                                    S.op("pe", ["xnT", wn], ["ps%d" % bank], lambda bank=bank, bi=bi, blk=blk, k=k, wsrc=wsrc: nc.tensor.matmul(
                                        ps[:, bank, bi * 128:(bi + 1) * 128], lhsT=wsrc[:, k, blk * 128:(blk + 1) * 128],
                                        rhs=xnT[:, k, :], start=(k == 0), stop=(k == 7)), inc=(k == 7 and bi == 3))
                            S.op("act", ["ps%d" % bank], [dn], lambda bank=bank, hb=hb, dst=dst, func=func, scale=scale: nc.scalar.activation(
                                out=dst[:, hb * 4:(hb + 1) * 4, :].rearrange("p a t -> p (a t)"), in_=ps[:, bank, :], func=func, scale=scale))
                    if STOP == 2:
                        return
                    for kv in range(4):
                        for k in range(8):
                            S.op("pe", ["xnT", "wkd"], ["ps1"], lambda kv=kv, k=k: nc.tensor.matmul(
                                ps[:, 1, kv * 128:(kv + 1) * 128], lhsT=wkd[:, k, kv * 128:(kv + 1) * 128],
                                rhs=xnT[:, k, :], start=(k == 0), stop=(k == 7)), inc=(k == 7 and kv == 3))
                    S.op("act", ["ps1"], [kTn], lambda p3=p3: nc.scalar.activation(
                        out=kT[p3][:].rearrange("p a t -> p (a t)"), in_=ps[:, 1, :], func=AF.Copy))
                    if STOP == 3:
                        return
                    for k in range(8):
                        S.op("pe", ["xnT", "wkv"], ["ps0"], lambda k=k: nc.tensor.matmul(
                            ps[:, 0, :], lhsT=xnT[:, k, :], rhs=wkv[:, k, :], start=(k == 0), stop=(k == 7)), inc=(k == 7))
                    S.op("dve", ["ps0"], [vn], lambda p3=p3: nc.vector.tensor_copy(
                        out=vtok[p3][:].rearrange("p a d -> p (a d)"), in_=ps[:, 0, 256:512]))
                    if t >= NT - 1 and "kvout" not in SKIP:
                        S.op("act", ["ps0"], ["kvout"], lambda: nc.scalar.activation(out=kvout[:], in_=ps[:, 0, :], func=AF.Copy))
                    if t == NT - 1 and "kwp" not in SKIP:
                        S.op("sp", ["kvout"], ["kwp"], lambda: nc.sync.dma_start(out=kwp, in_=kvout[:, 0:256]), dsem="o")
                        S.op("sp", ["kvout"], ["vwp"], lambda: nc.sync.dma_start(out=vwp, in_=kvout[:, 256:512]), dsem="o")

                def l0_pre(t):
                    is_s = (t == NT)
                    par = t % 2
                    p3 = t % 3
                    pv3 = (t - 1) % 3
                    xt = xts[p3]
                    xtn = "xt%d" % p3
                    qT, qTn = qTs[par], "qT%d" % par
                    gT, gTn = gTs[par], "gT%d" % par
                    kTn = "kT%d" % p3
                    vn = "vtok%d" % p3
                    if is_s:
                        for b in range(4):
                            ckd5 = ckdup[:].rearrange("p v (r d) -> p v r d", r=2)
                            for r in range(2):
                                S.op("dve", ["ckf%d" % b], ["ckdup"], lambda b=b, r=r: nc.vector.tensor_copy(
                                    out=ckd5[:, :, r, :], in_=ckf[b][:].rearrange("p (v d) -> p v d", v=4)))
                            pT = psb(2)
                            for kv in range(4):
                                S.op("pe", ["ckdup", "identb"], ["ps2"], lambda kv=kv, pT=pT: nc.tensor.transpose(
                                    out=pT[:, kv * 128:(kv + 1) * 128], in_=ckdup[:, kv, :], identity=identb[:]), inc=(kv == 3))
                            S.op("act", ["ps2"], ["kTc%d" % b], lambda b=b, pT=pT: nc.scalar.activation(
                                out=kTc[b][:].rearrange("p a t -> p (a t)"), in_=pT[:, 0:512], func=AF.Copy))
                            S.op("dve", ["cvf%d" % b], ["cvb%d" % b], lambda b=b: nc.vector.tensor_copy(
                                out=cvb[b][:].rearrange("p a d -> p (a d)"), in_=cvf[b][:]))
                            S.op("sp", ["ckf%d" % b], ["kws"], lambda b=b: nc.sync.dma_start(out=kws[b, 0:96, :], in_=ckf[b][32:128, :]), dsem="o")
                            S.op("sp", ["cvf%d" % b], ["vws"], lambda b=b: nc.sync.dma_start(out=vws[b, 0:96, :], in_=cvf[b][32:128, :]), dsem="o")
                            S.op("sp", ["kvout"], ["kws"], lambda b=b: nc.sync.dma_start(out=kws[b, 96:128, :], in_=kvout[b * 32:(b + 1) * 32, 0:256]), dsem="o")
                            S.op("sp", ["kvout"], ["vws"], lambda b=b: nc.sync.dma_start(out=vws[b, 96:128, :], in_=kvout[b * 32:(b + 1) * 32, 256:512]), dsem="o")

                def l0_chunk(t, c):
                    is_s = (t == NT)
                    par = t % 2
                    p3 = t % 3
                    pv3 = (t - 1) % 3
                    xt = xts[p3]
                    xtn = "xt%d" % p3
                    qT, qTn = qTs[par], "qT%d" % par
                    gT, gTn = gTs[par], "gT%d" % par
                    kTn = "kT%d" % p3
                    vn = "vtok%d" % p3
                    cs = CS[c]
                    if not is_s:
                        blocks = []
                        cur = dict(kT=kT[p3], kn=kTn, v=vtok[p3], vn=vn)
                        prv = dict(kT=kT[pv3], kn="kT%d" % pv3, v=vtok[pv3], vn="vtok%d" % pv3)
                        if c == 0:
                            if t > 0:
                                blocks.append(dict(prv, tab=(tabs, 0)))
                            blocks.append(dict(cur, tab=(tabs, 2)))
                        else:
                            blocks.append(dict(cur, tab=(tabs, 4)))
                            if t > 0:
                                blocks.append(dict(prv, tab=(tabs, 6)))
                        attend(c * 64, 64, blocks, qT, qTn, gT, gTn, cs)
                    else:
                        for b in (c, c + 2):
                            blocks = [dict(kT=kTc[b], kn="kTc%d" % b, v=cvb[b], vn="cvb%d" % b, tab=(tabs2, 0)),
                                      dict(kT=kT[p3], kn=kTn, v=vtok[p3], vn=vn, tab=(tabs2, 2 + 2 * b))]
                            attend(b * 32, 32, blocks, qT, qTn, gT, gTn, cs)

                def l0_post(t):
                    is_s = (t == NT)
                    par = t % 2
                    p3 = t % 3
                    pv3 = (t - 1) % 3
                    xt = xts[p3]
                    xtn = "xt%d" % p3
                    qT, qTn = qTs[par], "qT%d" % par
                    gT, gTn = gTs[par], "gT%d" % par
                    kTn = "kT%d" % p3
                    vn = "vtok%d" % p3
                    x1 = x1t[par]
                    x1n = "x1t%d" % par
                    ogn = ["ogT%d_%d" % (kvg_, q_) for kvg_ in range(2) for q_ in ((0, 32, 64, 96) if is_s else (0, 64))]
                    for nh in range(2):
                        bk = 2 if nh == 0 else 5
                        for pr in range(8):
                            S.op("pe", ogn + ["wo"], ["ps%d" % bk], lambda nh=nh, pr=pr, bk=bk: nc.tensor.matmul(
                                ps[:, bk, :], lhsT=ogT[:, pr, :], rhs=wo[:, pr, nh * 512:(nh + 1) * 512],
                                start=(pr == 0), stop=(pr == 7)), inc=(pr == 7))
                        S.op("dve", ["ps%d" % bk, xtn], [x1n], lambda nh=nh, x1=x1, xt=xt, bk=bk: nc.vector.tensor_tensor(
                            out=x1[:, nh * 512:(nh + 1) * 512], in0=ps[:, bk, :], in1=xt[:, nh * 512:(nh + 1) * 512], op=ALU.add))
                    S.op("sp", [x1n], ["x1d_%d" % t], lambda x1=x1, t=t: nc.sync.dma_start(
                        out=x1d[t * 128:(t + 1) * 128, :], in_=x1[:]), dsem="st" + x1n)

                def rec_ops(fn_, *a_):
                    S.rec = []
                    fn_(*a_)
                    ops_ = S.rec
                    S.rec = None
                    return ops_

                def to_steps0(ops_):
                    steps_ = []
                    for o_ in ops_:
                        if steps_ and steps_[-1][-1][0] == o_[0] and o_[0] == "pe" and len(steps_[-1]) < PESTEP:
                            steps_[-1].append(o_)
                        else:
                            steps_.append([o_])
                    return steps_

                def emit0(ops_):
                    for (eng_, r_, w_, fn_, inc_, ds_) in ops_:
                        S.op(eng_, r_, w_, fn_, inc=inc_, dsem=ds_)

                l0_load(0)
                l0_load(1)
                PL0 = Planner()

                def emit0p(ops_):
                    for st_ in to_steps0(ops_):
                        PL0.commit(st_)
                        emit0(st_)

                emit0p(rec_ops(l0_front, 0))
                for t in range(NT + 1):
                    S.rec = []
                    l0_load(t + 2)
                    ld_ = S.rec
                    S.rec = None
                    emit0p(ld_)
                    emit0p(rec_ops(l0_pre, t))
                    stC0_ = to_steps0(rec_ops(l0_chunk, t, 0))
                    stC1_ = to_steps0(rec_ops(l0_chunk, t, 1))
                    stZ_ = [[("gate", 1, len(stC0_))], [("gate", 2, len(stC1_))]] + to_steps0(rec_ops(l0_post, t))
                    stF_ = to_steps0(rec_ops(l0_front, t + 1)) if t < NT else []
                    PL0.merge([stF_, stC0_, stC1_, stZ_], emit0)
            S.barrier()

        if layers >= 2:
          with ExitStack() as e1:
            wqkv = sb(e1, "wqkv", [128, 8, 3072], BF16)
            wz = sb(e1, "wz", [128, 8, 1024], BF16)
            wba = sb(e1, "wba", [128, 8, 16], BF16)
            wo1 = sb(e1, "wo1", [128, 8, 1024], BF16)
            diag = sb(e1, "diag", [128, 96, 128], BF16)
            cws = sb(e1, "cws", [128, 96], F32)
            masks = sb(e1, "masks", [128, 7, 128], F32)
            fgt = sb(e1, "fgt", [128, D], F32)
            negA = sb(e1, "negA", [128, 8], F32)
            dtb = sb(e1, "dtb", [128, 8], F32)
            poshalf = sb(e1, "poshalf", [128, 16], F32)
            onesf = sb(e1, "onesf", [128, 128], F32)
            S.op("sp", [], ["cws"], lambda: nc.sync.dma_start(out=cws[:], in_=cwl), dsem="c")
            S.op("sp", [], ["masks"], lambda: nc.sync.dma_start(out=masks[:], in_=masksd), dsem="c")
            S.op("sp", [], ["fgt"], lambda: nc.sync.dma_start(out=fgt[:], in_=fgb), dsem="c")
            S.op("sp", [], ["negA"], lambda: nc.sync.dma_start(out=negA[:], in_=alogb), dsem="c")
            S.op("sp", [], ["dtb"], lambda: nc.sync.dma_start(out=dtb[:], in_=dtbb), dsem="c")
            S.op("act", ["negA"], ["negA"], lambda: nc.scalar.activation(out=negA[:], in_=negA[:], func=AF.Exp))
            S.op("dve", ["negA"], ["negA"], lambda: nc.vector.tensor_scalar(
                out=negA[:], in0=negA[:], scalar1=-1.0, scalar2=None, op0=ALU.mult))
            S.op("dve", [], ["poshalf"], lambda: nc.vector.memset(poshalf[:], 0.5))
            S.op("dve", [], ["onesf"], lambda: nc.vector.memset(onesf[:], 1.0))
            S.op("dve", ["cws", "ident"], ["diag"], lambda: nc.vector.tensor_tensor(
                out=diag[:], in0=ident[:, :].unsqueeze(1).broadcast_to([128, 96, 128]),
                in1=cws[:, :].unsqueeze(2).broadcast_to([128, 96, 128]), op=ALU.mult))

            with ExitStack() as ew:
                stg = [sb(ew, "stg0b", [128, 8, 512], F32), sb(ew, "stg1b", [128, 8, 512], F32)]

                def dst1(c0, c1, st, nm, eng):
                    if c0 < 3072:
                        scaled_cast(eng, wqkv[:, :, c0:c1], st, nm, 512, 1, "wqkv")
                    elif c0 < 4096:
                        scaled_cast(eng, wz[:, :, c0 - 3072:c1 - 3072], st, nm, 512, 1, "wz")
                    else:
                        scaled_cast(eng, wba[:, :, :], st, nm, 16, 1, "wba")

                load_weight(stg, w_in1, 4112, dst1, 1, "wi1")
                load_weight(stg, w_out1, 1024,
                            lambda c0, c1, st, nm, eng: scaled_cast(eng, wo1[:, :, c0:c1], st, nm, 512, 2, "wo1"),
                            None, "wo1")
                S.barrier()

            xts = [sb(e1, "yt0", [128, D], F32), sb(e1, "yt1", [128, D], F32)]
            yo0_ = sb(e1, "yo0", [128, D], F32)
            yo = [yo0_, yo0_]
            ss = sb(e1, "ssb", [128, 1], F32)
            ms = sb(e1, "msb", [128, 1], F32)
            rstd = sb(e1, "rstdb", [128, 1], F32)
            xn = sb(e1, "xnb", [128, D], BF16)
            xnT = sb(e1, "xnTb", [128, 8, 128], BF16)
            junk = xn
            rawb1 = sb(e1, "rawb", [128, 24, 131], BF16)
            halo = sb(e1, "halo", [128, 24, 3], BF16)
            histfs = [sb(e1, "histf%d" % i, [128, 24, 3], F32) for i in range(2)]
            scf = sb(e1, "scf", [128, 24, 4, 3], F32)
            qkvcs = [sb(e1, "qkvc%d" % i, [128, 24, 128], BF16) for i in range(2)]
            sq = sb(e1, "sq", [128, 16, 128], BF16)
            zss = [sb(e1, "zs0", [128, D], BF16), sb(e1, "zs1", [128, D], BF16)]
            kg = sb(e1, "kg", [128, 8, 128], BF16)
            kdec = sb(e1, "kdec", [128, 8, 128], BF16)
            vt = sb(e1, "vt", [128, 8, 128], BF16)
            scs = [sb(e1, "sc0", [128, 20, 8], F32), sb(e1, "sc1", [128, 20, 8], F32)]
            ss16s = [sb(e1, "ss16_%d" % i, [128, 16], F32) for i in range(2)]
            r16 = sb(e1, "r16", [128, 16], F32)
            Ef = sb(e1, "Ef", [128, 8, 128], F32)
            tA = sb(e1, "tA", [128, 8, 128], F32)
            Pb = [sb(e1, "Pb0", [128, 8, 128], BF16), sb(e1, "Pb1", [128, 8, 128], BF16)]
            Qb = [sb(e1, "Qb0", [128, 8, 128], BF16), sb(e1, "Qb1", [128, 8, 128], BF16)]
            Xf = sb(e1, "Xf", [128, 8, 128], F32)
            Xb = sb(e1, "Xb", [128, 8, 128], BF16)
            intraT = sb(e1, "intraT", [128, 8, 128], BF16)
            Sf = sb(e1, "Sf", [128, 8, 128], F32)
            Sbf = sb(e1, "Sbf", [128, 8, 128], BF16)
            wT = sb(e1, "wT", [128, 8, 128], BF16)
            vnew = sb(e1, "vnewb", [128, 8, 128], BF16)
            of = sb(e1, "of", [128, 8, 128], F32)
            ogb = sb(e1, "ogb", [128, D], BF16)
            ogT = sb(e1, "ogTb", [128, 8, 128], BF16)
            sso = sb(e1, "sso", [128, 8], F32)
            S.op("sp", [], ["scf"], lambda: nc.sync.dma_start(out=scf[:], in_=sconv), dsem="c")

            (I_B, I_A, I_E1, I_BETA, I_T, I_G, I_GC, I_GL, I_EGC, I_EDEC, I_EGL, I_DSC, I_AJ, I_CKDEC,
             I_CVT, I_C5, I_C4, I_RK, I_RQ, I_TMP) = range(20)

            def scv(i):
                return sc[:, i, :]

            def bc(ap8):
                return ap8.unsqueeze(2).broadcast_to([128, 8, 128])

            def mbc(mi):
                return masks[:, mi, :].unsqueeze(1).broadcast_to([128, 8, 128])

            def hb(bank0, h):
                return ps[:, bank0 + h // 4, (h % 4) * 128:(h % 4 + 1) * 128]

            def ps2b(bank0):
                return ps[:, bank0:bank0 + 2, :].rearrange("p b (h d) -> p (b h) d", d=128)

            tiles = [("p", t) for t in range(NT)] + [("s", b) for b in range(4)]

            def prefix_ops(ti):
                kind, idx = tiles[ti]
                par = ti % 2
                pz = "p%d" % par
                sc = scs[par]
                zs = zss[par]
                is_s = (kind == "s")
                nv = 32 if is_s else 128
                first = is_s or idx == 0
                last = is_s or idx == NT - 1
                ops = []

                def Sop(eng, reads, writes, fn, inc=True, dsem=None):
                    ops.append((eng, reads, writes, fn, inc, dsem))

                def scv(i):
                    return sc[:, i, :]

                xt = xts[par]
                xtn = "yt%d" % par
                if is_s:
                    Sop("dve", [], [xtn], lambda xt=xt: nc.vector.memset(xt[:], 0.0))
                    r0 = NT * 128 + idx * 32
                    Sop("sp", ["x1d_%d" % NT], [xtn], lambda xt=xt, r0=r0: nc.sync.dma_start(out=xt[0:32, :], in_=x1d[r0:r0 + 32, :]), dsem="ld" + xtn)
                else:
                    Sop("sp", ["x1d_%d" % idx], [xtn], lambda xt=xt, idx=idx: nc.sync.dma_start(out=xt[:], in_=x1d[idx * 128:(idx + 1) * 128, :]), dsem="ld" + xtn)
                    if idx == 0:
                        Sop("dve", [], ["Sf_0", "Sf_1"], lambda: nc.vector.memset(Sf[:], 0.0))
                        Sop("dve", [], ["Sbf_0", "Sbf_1"], lambda: nc.vector.memset(Sbf[:], 0.0))
                rb = rawb1
                if is_s:
                    Sop("pool", ["scf"], ["rawbhalo"], lambda rb=rb, idx=idx: nc.gpsimd.tensor_copy(out=rb[:, :, 0:3], in_=scf[:, :, idx, :]))
                elif idx == 0:
                    Sop("pool", [], ["rawbhalo"], lambda rb=rb: nc.gpsimd.memset(rb[:, :, 0:3], 0.0))
                else:
                    Sop("pool", ["halo_0", "halo_1"], ["rawbhalo"], lambda rb=rb: nc.gpsimd.tensor_copy(out=rb[:, :, 0:3], in_=halo[:]))

                rms_and_transpose(xt, xtn, "ss", xn, xnT, (junk, ss, ms, rstd), jn="xn", emit=Sop)
                for k in range(8):
                    Sop("pe", ["xnT", "wba"], ["ps7"], lambda k=k: nc.tensor.matmul(
                        ps[:, 7, 0:16], lhsT=xnT[:, k, :], rhs=wba[:, k, :], start=(k == 0), stop=(k == 7)), inc=(k == 7))
                Sop("dve", ["ps7"], ["sc_ba" + pz], lambda: nc.vector.tensor_copy(out=sc[:, I_B:I_B + 2, :].rearrange("p a h -> p (a h)"), in_=ps[:, 7, 0:16]))
                Sop("act", ["sc_ba" + pz], ["sc_e1" + pz], lambda: nc.scalar.activation(out=scv(I_E1), in_=scv(I_B), func=AF.Exp, scale=-1.0))
                Sop("dve", ["sc_e1" + pz], ["sc_e1" + pz], lambda: nc.vector.tensor_scalar(out=scv(I_E1), in0=scv(I_E1), scalar1=1.0, scalar2=None, op0=ALU.add))
                Sop("dve", ["sc_e1" + pz], ["sc_beta" + pz], lambda: nc.vector.reciprocal(out=scv(I_BETA), in_=scv(I_E1)))
                Sop("dve", ["sc_ba" + pz, "dtb"], ["sc_t" + pz], lambda: nc.vector.tensor_tensor(out=scv(I_T), in0=scv(I_A), in1=dtb[:], op=ALU.add))
                Sop("act", ["sc_t" + pz], ["sc_t" + pz], lambda: nc.scalar.activation(out=scv(I_T), in_=scv(I_T), func=AF.Exp))
                Sop("dve", ["sc_t" + pz], ["sc_t" + pz], lambda: nc.vector.tensor_scalar(out=scv(I_T), in0=scv(I_T), scalar1=1.0, scalar2=None, op0=ALU.add))
                Sop("act", ["sc_t" + pz], ["sc_t" + pz], lambda: nc.scalar.activation(out=scv(I_T), in_=scv(I_T), func=AF.Ln))
                Sop("dve", ["sc_t" + pz, "negA"], ["sc_g" + pz], lambda: nc.vector.tensor_tensor(out=scv(I_G), in0=scv(I_T), in1=negA[:], op=ALU.mult))
                if is_s:
                    Sop("dve", ["sc_g" + pz, "masks"], ["sc_g" + pz], lambda: nc.vector.tensor_scalar(
                        out=scv(I_G), in0=scv(I_G), scalar1=masks[:, 4, 0:1], scalar2=None, op0=ALU.mult))
                Sop("pe", ["sc_g" + pz, "masks"], ["ps7"], lambda: nc.tensor.matmul(
                    ps[:, 7, 16:24], lhsT=masks[:, 0, :], rhs=scv(I_G), start=True, stop=True), inc=False)
                Sop("pe", ["sc_g" + pz, "onesf"], ["ps7"], lambda: nc.tensor.matmul(
                    ps[:, 7, 24:32], lhsT=onesf[:], rhs=scv(I_G), start=True, stop=True), inc=True)
                Sop("dve", ["ps7"], ["sc_gc" + pz], lambda: nc.vector.tensor_copy(out=sc[:, I_GC:I_GC + 2, :].rearrange("p a h -> p (a h)"), in_=ps[:, 7, 16:32]))
                Sop("act", ["sc_gc" + pz], ["sc_egc" + pz], lambda: nc.scalar.activation(out=scv(I_EGC), in_=scv(I_GC), func=AF.Exp))
                Sop("act", ["sc_gc" + pz], ["sc_egl" + pz], lambda: nc.scalar.activation(out=scv(I_EGL), in_=scv(I_GL), func=AF.Exp))
                Sop("dve", ["sc_gc" + pz], ["sc_edec" + pz], lambda: nc.vector.tensor_tensor(out=scv(I_EDEC), in0=scv(I_GL), in1=scv(I_GC), op=ALU.subtract))
                Sop("act", ["sc_edec" + pz], ["sc_edec" + pz], lambda: nc.scalar.activation(out=scv(I_EDEC), in_=scv(I_EDEC), func=AF.Exp))
                for nh in range(2):
                    for k in range(8):
                        Sop("pe", ["xnT", "wz"], ["ps7"], lambda nh=nh, k=k: nc.tensor.matmul(
                            ps[:, 7, :], lhsT=xnT[:, k, :], rhs=wz[:, k, nh * 512:(nh + 1) * 512], start=(k == 0), stop=(k == 7)), inc=(k == 7))
                    Sop("act", ["ps7"], ["zs" + pz], lambda nh=nh: nc.scalar.activation(
                        out=zs[:, nh * 512:(nh + 1) * 512], in_=ps[:, 7, :], func=AF.Silu))
                return ops

            def gstream(ti, g, front=False):
                kind, idx = tiles[ti]
                par = ti % 2
                pz = "p%d" % par
                sc = scs[par]
                zs = zss[par]
                qkvc = qkvcs[par]
                ss16 = ss16s[par]
                histf = histfs[par]
                is_s = (kind == "s")
                nv = 32 if is_s else 128
                last = is_s or idx == NT - 1
                rb = rawb1
                if True:
                    sfx = "_%d" % g
                    ops = []
                    proj_end = [0]

                    def Sop(eng, reads, writes, fn, inc=True, dsem=None):
                        ops.append((eng, reads, writes, fn, inc, dsem))

                    def scv(i):
                        return sc[:, i, :]
                    pbk, cbk = (5, 6) if front else (1 + g, 3 + g)
                    tbk = 0 if g == 0 else 7
                    PB, CB = "ps%d" % pbk, "ps%d" % cbk
                    h0 = 4 * g

                    def G4(t):
                        return t[:, h0:h0 + 4, :]

                    def sg(i):
                        return sc[:, i, h0:h0 + 4]

                    def bc4(ap4):
                        return ap4.unsqueeze(2).broadcast_to([128, 4, 128])

                    def mbc4(mi):
                        return masks[:, mi, :].unsqueeze(1).broadcast_to([128, 4, 128])

                    def hbk(bank, hi):
                        return ps[:, bank, hi * 128:(hi + 1) * 128]

                    def p4(bank):
                        return ps[:, bank, :].rearrange("p (h d) -> p h d", d=128)

                    for which in range(3):
                        blk0 = which * 8 + h0
                        rn = "rawb%d%s" % (which, sfx)
                        qn = "qkvc%d%s%s" % (which, pz, sfx)
                        for bi in range(4):
                            blk = blk0 + bi
                            for k in range(8):
                                Sop("pe", ["xnT", "wqkv"], [PB], lambda bi=bi, blk=blk, k=k: nc.tensor.matmul(
                                    ps[:, pbk, bi * 128:(bi + 1) * 128], lhsT=wqkv[:, k, blk * 128:(blk + 1) * 128],
                                    rhs=xnT[:, k, :], start=(k == 0), stop=(k == 7)), inc=(k == 7 and bi == 3))
                        Sop("act", [PB], [rn], lambda blk0=blk0: nc.scalar.activation(
                            out=rb[:, blk0:blk0 + 4, 3:131], in_=ps[:, pbk, :].rearrange("p (a t) -> p a t", a=4), func=AF.Copy))
                        if last:
                            Sop("dve", [PB], ["histf" + pz + sfx], lambda blk0=blk0: nc.vector.tensor_copy(
                                out=histf[:, blk0:blk0 + 4, :], in_=ps[:, pbk, :].rearrange("p (a t) -> p a t", a=4)[:, :, nv - 3:nv]))
                        for bi in range(4):
                            blk = blk0 + bi
                            for j in range(4):
                                Sop("pe", ["rawbhalo", rn, "diag"], [CB], lambda bi=bi, blk=blk, j=j: nc.tensor.matmul(
                                    ps[:, cbk, bi * 128:(bi + 1) * 128], lhsT=diag[:, blk * 4 + j, :], rhs=rb[:, blk, j:j + 128],
                                    start=(j == 0), stop=(j == 3)), inc=(j == 3 and bi == 3))
                        Sop("act", [CB], [qn], lambda blk0=blk0: nc.scalar.activation(
                            out=qkvc[:, blk0:blk0 + 4, :].rearrange("p a t -> p (a t)"), in_=ps[:, cbk, :], func=AF.Silu))
                        if which < 2:
                            Sop("act", [qn], ["sq%d%s" % (which, sfx)], lambda blk0=blk0: nc.scalar.activation(
                                out=sq[:, blk0:blk0 + 4, :], in_=qkvc[:, blk0:blk0 + 4, :], func=AF.Square))
                    if not last:
                        Sop("pool", ["rawb%d%s" % (w_, sfx) for w_ in range(3)], ["halo" + sfx], lambda: nc.gpsimd.tensor_copy(
                            out=halo[:].rearrange("p (w g b) t -> p w g b t", w=3, g=2)[:, :, g, :, :],
                            in_=rb[:].rearrange("p (w g b) t -> p w g b t", w=3, g=2)[:, :, g, :, 128:131]))
                    QN, KN, VN = "qkvc0" + pz + sfx, "qkvc1" + pz + sfx, "qkvc2" + pz + sfx
                    for which in range(2):
                        for bi in range(4):
                            blk = which * 8 + h0 + bi
                            Sop("pe", ["sq%d%s" % (which, sfx), "onesb"], [CB], lambda blk=blk, which=which, bi=bi: nc.tensor.matmul(
                                ps[:, cbk, which * 4 + bi:which * 4 + bi + 1], lhsT=sq[:, blk, :], rhs=onesb[:, 0:1], start=True, stop=True),
                                inc=(which == 1 and bi == 3))
                    ss16v = ss16[:].rearrange("p (w h) -> p w h", w=2)[:, :, h0:h0 + 4]
                    r16v = r16[:].rearrange("p (w h) -> p w h", w=2)[:, :, h0:h0 + 4]
                    Sop("dve", [CB], ["ss16" + pz + sfx], lambda: nc.vector.tensor_scalar(
                        out=ss16v, in0=ps[:, cbk, 0:8].rearrange("p (w h) -> p w h", w=2),
                        scalar1=EPS, scalar2=None, op0=ALU.add))
                    proj_end[0] = len(ops)
                    if is_s:
                        Sop("sp", [], ["Sf" + sfx], lambda: nc.sync.dma_start(
                            out=Sf[:, h0:h0 + 4, :], in_=sssm[idx, h0:h0 + 4].rearrange("h k v -> k h v")), dsem="sld%d" % g)
                        Sop("act", ["Sf" + sfx], ["Sbf" + sfx], lambda: nc.scalar.activation(out=G4(Sbf), in_=G4(Sf), func=AF.Copy))
                    Sop("pool", ["ss16" + pz + sfx, "neghalf"], ["r16" + sfx], lambda: nc.gpsimd.tensor_tensor(
                        out=r16v, in0=ss16v, in1=neghalf[:, 0:8].rearrange("p (w h) -> p w h", w=2), op=ALU.pow))
                    rk = r16[:, 8 + h0:8 + h0 + 4]
                    rq = r16[:, h0:h0 + 4]
                    Sop("pool", ["ss16" + pz + sfx, "poshalf"], ["sc_cvt" + pz + sfx], lambda: nc.gpsimd.tensor_tensor(
                        out=sg(I_CVT), in0=ss16[:, 8 + h0:8 + h0 + 4], in1=poshalf[:, 0:4], op=ALU.pow))
                    Sop("dve", ["r16" + sfx, "sc_beta" + pz], ["sc_dsc" + pz + sfx], lambda: nc.vector.tensor_tensor(out=sg(I_DSC), in0=sg(I_BETA), in1=rk, op=ALU.mult))
                    Sop("dve", ["r16" + sfx, "sc_dsc" + pz + sfx], ["sc_aj" + pz + sfx], lambda: nc.vector.tensor_tensor(out=sg(I_AJ), in0=sg(I_DSC), in1=rk, op=ALU.mult))
                    Sop("dve", ["r16" + sfx, "sc_edec" + pz], ["sc_ckdec" + pz + sfx], lambda: nc.vector.tensor_tensor(out=sg(I_CKDEC), in0=sg(I_EDEC), in1=rk, op=ALU.mult))
                    if is_s:
                        Sop("dve", ["sc_ckdec" + pz + sfx, "masks"], ["sc_ckdec" + pz + sfx], lambda: nc.vector.tensor_scalar(
                            out=sg(I_CKDEC), in0=sg(I_CKDEC), scalar1=masks[:, 4, 0:1], scalar2=None, op0=ALU.mult))
                    Sop("dve", ["r16" + sfx], ["sc_c5" + pz + sfx], lambda: nc.vector.tensor_scalar(
                        out=sg(I_C5), in0=rq, scalar1=float(128 ** -0.5), scalar2=None, op0=ALU.mult))
                    Sop("dve", ["sc_c5" + pz + sfx, "sc_egc" + pz], ["sc_c4" + pz + sfx], lambda: nc.vector.tensor_tensor(out=sg(I_C4), in0=sg(I_C5), in1=sg(I_EGC), op=ALU.mult))
                    pTk = psb(pbk)
                    for hi in range(4):
                        Sop("pe", [KN, "identb"], [PB], lambda hi=hi: nc.tensor.transpose(
                            out=pTk[:, hi * 128:(hi + 1) * 128], in_=qkvc[:, 8 + h0 + hi, :], identity=identb[:]), inc=(hi == 3))
                    pTkv = pTk[:, 0:512].rearrange("p (h d) -> p h d", h=4)
                    Sop("dve", [PB, "sc_egc" + pz], ["kg" + sfx], lambda: nc.vector.tensor_tensor(out=G4(kg), in0=pTkv, in1=bc4(sg(I_EGC)), op=ALU.mult))
                    Sop("dve", [PB, "sc_ckdec" + pz + sfx], ["kdec" + sfx], lambda: nc.vector.tensor_tensor(out=G4(kdec), in0=pTkv, in1=bc4(sg(I_CKDEC)), op=ALU.mult))
                    pTv = psb(cbk)
                    for hi in range(4):
                        Sop("pe", [VN, "identb"], [CB], lambda hi=hi: nc.tensor.transpose(
                            out=pTv[:, hi * 128:(hi + 1) * 128], in_=qkvc[:, 16 + h0 + hi, :], identity=identb[:]), inc=(hi == 3))
                    Sop("dve", [CB, "sc_cvt" + pz + sfx], ["vt" + sfx], lambda: nc.vector.tensor_tensor(
                        out=G4(vt), in0=pTv[:, 0:512].rearrange("p (h d) -> p h d", h=4), in1=bc4(sg(I_CVT)), op=ALU.mult))
                    Sop("dve", ["sc_g" + pz, "masks"], ["Xf" + sfx], lambda: nc.vector.tensor_tensor(out=G4(Xf), in0=mbc4(1), in1=bc4(sg(I_G)), op=ALU.mult))
                    for hi in range(4):
                        Sop("pe", ["Xf" + sfx, "masks"], [PB], lambda hi=hi: nc.tensor.matmul(
                            hbk(pbk, hi), lhsT=Xf[:, h0 + hi, :], rhs=masks[:, 0, :], start=True, stop=True), inc=(hi == 3))
                    Sop("act", [PB], ["Ef" + sfx], lambda: nc.scalar.activation(out=G4(Ef), in_=p4(pbk), func=AF.Exp))
                    for hi in range(4):
                        Sop("pe", [KN], [PB], lambda hi=hi: nc.tensor.matmul(
                            hbk(pbk, hi), lhsT=qkvc[:, 8 + h0 + hi, :], rhs=qkvc[:, 8 + h0 + hi, :], start=True, stop=True), inc=(hi == 3))
                    for hi in range(4):
                        Sop("pe", [KN, QN], [CB], lambda hi=hi: nc.tensor.matmul(
                            hbk(cbk, hi), lhsT=qkvc[:, 8 + h0 + hi, :], rhs=qkvc[:, h0 + hi, :], start=True, stop=True), inc=(hi == 3))
                    Sop("pool", ["Ef" + sfx, "masks"], ["tA" + sfx], lambda: nc.gpsimd.tensor_tensor(out=G4(tA), in0=G4(Ef), in1=mbc4(2), op=ALU.mult))
                    Sop("pool", ["tA" + sfx, "sc_aj" + pz + sfx], ["tA" + sfx], lambda: nc.gpsimd.tensor_tensor(out=G4(tA), in0=G4(tA), in1=bc4(sg(I_AJ)), op=ALU.mult))
                    Sop("dve", [PB, "tA" + sfx], ["Xf" + sfx], lambda: nc.vector.tensor_tensor(out=G4(Xf), in0=p4(pbk), in1=G4(tA), op=ALU.mult))
                    Sop("act", ["Xf" + sfx], ["wT" + sfx], lambda: nc.scalar.activation(out=G4(wT), in_=G4(Xf), func=AF.Copy))
                    Sop("dve", ["Xf" + sfx, "masks"], ["Xf" + sfx], lambda: nc.vector.tensor_tensor(out=G4(Xf), in0=G4(Xf), in1=mbc4(4), op=ALU.mult))
                    Sop("act", ["Xf" + sfx], ["Pb0" + sfx], lambda: nc.scalar.activation(out=G4(Pb[0]), in_=G4(Xf), func=AF.Copy))
                    Sop("dve", ["Xf" + sfx, "ident"], ["Xf" + sfx], lambda: nc.vector.tensor_tensor(
                        out=G4(Xf), in0=ident[:, :].unsqueeze(1).broadcast_to([128, 4, 128]), in1=G4(Xf), op=ALU.subtract))
                    Sop("act", ["Xf" + sfx], ["Xb" + sfx], lambda: nc.scalar.activation(out=G4(Xb), in_=G4(Xf), func=AF.Copy))
                    Sop("pool", ["Ef" + sfx, "masks"], ["Ef" + sfx], lambda: nc.gpsimd.tensor_tensor(out=G4(Ef), in0=G4(Ef), in1=mbc4(3), op=ALU.mult))
                    Sop("pool", ["Ef" + sfx, "r16" + sfx], ["Ef" + sfx], lambda: nc.gpsimd.tensor_tensor(out=G4(Ef), in0=G4(Ef), in1=bc4(rk), op=ALU.mult))
                    Sop("dve", [CB, "Ef" + sfx], ["intraT" + sfx], lambda: nc.vector.tensor_tensor(out=G4(intraT), in0=p4(cbk), in1=G4(Ef), op=ALU.mult))
                    pTq = psb(cbk)
                    for hi in range(4):
                        Sop("pe", ["wT" + sfx, "identb"], [CB], lambda hi=hi: nc.tensor.transpose(
                            out=pTq[:, hi * 128:(hi + 1) * 128], in_=wT[:, h0 + hi, :], identity=identb[:]), inc=(hi == 3))
                    pTq4 = pTq[:, 0:512].rearrange("p (h d) -> p h d", h=4)
                    Sop("dve", [CB, "masks"], ["Qb0" + sfx], lambda: nc.vector.tensor_tensor(out=G4(Qb[0]), in0=pTq4, in1=mbc4(4), op=ALU.mult))
                    Sop("act", [CB], ["wT" + sfx], lambda: nc.scalar.activation(out=G4(wT), in_=pTq4, func=AF.Copy))
                    Sop("pool", ["wT" + sfx, "masks"], ["vnew" + sfx], lambda: nc.gpsimd.tensor_tensor(out=G4(vnew), in0=G4(wT), in1=mbc4(5), op=ALU.mult))
                    NLEV = 4
                    for s_ in range(1, NLEV + 1):
                        Pp, Qp = Pb[(s_ - 1) % 2], Qb[(s_ - 1) % 2]
                        Pn, Qn = Pb[s_ % 2], Qb[s_ % 2]
                        Ppn, Qpn = "Pb%d%s" % ((s_ - 1) % 2, sfx), "Qb%d%s" % ((s_ - 1) % 2, sfx)
                        Pnn, Qnn = "Pb%d%s" % (s_ % 2, sfx), "Qb%d%s" % (s_ % 2, sfx)
                        for hi in range(4):
                            Sop("pe", [Ppn, Qpn], [CB], lambda hi=hi, Pp=Pp, Qp=Qp: nc.tensor.matmul(
                                hbk(cbk, hi), lhsT=Pp[:, h0 + hi, :], rhs=Qp[:, h0 + hi, :], start=True, stop=True), inc=(hi == 3))
                        if s_ < NLEV:
                            for hi in range(4):
                                Sop("pe", [Ppn, Qpn], [PB], lambda hi=hi, Pp=Pp, Qp=Qp: nc.tensor.matmul(
                                    hbk(pbk, hi), lhsT=Qp[:, h0 + hi, :], rhs=Pp[:, h0 + hi, :], start=True, stop=True), inc=(hi == 3))
                        Sop("act", [CB], [Qnn], lambda Qn=Qn: nc.scalar.activation(out=G4(Qn), in_=p4(cbk), func=AF.Copy))
                        if s_ < NLEV:
                            Sop("dve", [PB], [Pnn], lambda Pn=Pn: nc.vector.tensor_copy(out=G4(Pn), in_=p4(pbk)))
                        for hi in range(4):
                            Sop("pe", [Qnn, "Xb" + sfx], [PB], lambda hi=hi, Qn=Qn: nc.tensor.matmul(
                                hbk(pbk, hi), lhsT=Qn[:, h0 + hi, :], rhs=Xb[:, h0 + hi, :], start=True, stop=True), inc=(hi == 3))
                        Sop("dve", [PB, "Xf" + sfx], ["Xf" + sfx], lambda: nc.vector.tensor_tensor(out=G4(Xf), in0=p4(pbk), in1=G4(Xf), op=ALU.add))
                        Sop("act", ["Xf" + sfx], ["Xb" + sfx], lambda: nc.scalar.activation(out=G4(Xb), in_=G4(Xf), func=AF.Copy))
                    for mi in (5, 6):
                        pTx = psb(cbk)
                        for hi in range(4):
                            Sop("pe", ["Xb" + sfx, "identb"], [CB], lambda hi=hi, pTx=pTx: nc.tensor.transpose(
                                out=pTx[:, hi * 128:(hi + 1) * 128], in_=Xb[:, h0 + hi, :], identity=identb[:]), inc=(hi == 3))
                        Sop("act", [CB], ["Pb1" + sfx], lambda pTx=pTx: nc.scalar.activation(
                            out=G4(Pb[1]), in_=pTx[:, 0:512].rearrange("p (h d) -> p h d", h=4), func=AF.Copy))
                        for hi in range(4):
                            Sop("pe", ["vnew" + sfx, "Xb" + sfx], [PB], lambda hi=hi: nc.tensor.matmul(
                                hbk(pbk, hi), lhsT=vnew[:, h0 + hi, :], rhs=Xb[:, h0 + hi, :], start=True, stop=True), inc=(hi == 3))
                        Sop("dve", [PB], ["Pb0" + sfx], lambda: nc.vector.tensor_copy(out=G4(Pb[0]), in_=p4(pbk)))
                        if mi == 5:
                            Sop("pool", ["wT" + sfx, "masks"], ["vnew" + sfx], lambda: nc.gpsimd.tensor_tensor(out=G4(vnew), in0=G4(wT), in1=mbc4(6), op=ALU.mult))
                        for hi in range(4):
                            Sop("pe", ["Pb1" + sfx, "Pb0" + sfx], [PB], lambda hi=hi: nc.tensor.matmul(
                                hbk(pbk, hi), lhsT=Pb[1][:, h0 + hi, :], rhs=Pb[0][:, h0 + hi, :], start=True, stop=True), inc=(hi == 3))
                        Sop("dve", [PB, "Xf" + sfx], ["Xf" + sfx], lambda: nc.vector.tensor_tensor(out=G4(Xf), in0=G4(Xf), in1=p4(pbk), op=ALU.subtract))
                        Sop("act", ["Xf" + sfx], ["Xb" + sfx], lambda: nc.scalar.activation(out=G4(Xb), in_=G4(Xf), func=AF.Copy))
                    for hi in range(4):
                        Sop("pe", ["kg" + sfx, "Xb" + sfx], [PB], lambda hi=hi: nc.tensor.matmul(
                            hbk(pbk, hi), lhsT=kg[:, h0 + hi, :], rhs=Xb[:, h0 + hi, :], start=True, stop=True), inc=(hi == 3))
                    Sop("act", [PB], ["wT" + sfx], lambda: nc.scalar.activation(out=G4(wT), in_=p4(pbk), func=AF.Copy, scale=-1.0))
                    for hi in range(4):
                        Sop("pe", ["Xb" + sfx, "vt" + sfx], [CB], lambda hi=hi: nc.tensor.matmul(
                            hbk(cbk, hi), lhsT=Xb[:, h0 + hi, :], rhs=vt[:, h0 + hi, :], start=True, stop=False), inc=False)
                        Sop("pe", ["wT" + sfx, "Sbf" + sfx], [CB], lambda hi=hi: nc.tensor.matmul(
                            hbk(cbk, hi), lhsT=wT[:, h0 + hi, :], rhs=Sbf[:, h0 + hi, :], start=False, stop=True), inc=(hi == 3))
                    Sop("dve", [CB, "sc_dsc" + pz + sfx], ["vnew" + sfx], lambda: nc.vector.tensor_tensor(out=G4(vnew), in0=p4(cbk), in1=bc4(sg(I_DSC)), op=ALU.mult))
                    for hi in range(4):
                        Sop("pe", [QN, "Sbf" + sfx], [PB], lambda hi=hi: nc.tensor.matmul(
                            hbk(pbk, hi), lhsT=qkvc[:, h0 + hi, :], rhs=Sbf[:, h0 + hi, :], start=True, stop=True), inc=(hi == 3))
                    for hi in range(4):
                        Sop("pe", ["intraT" + sfx, "vnew" + sfx], [CB], lambda hi=hi: nc.tensor.matmul(
                            hbk(cbk, hi), lhsT=intraT[:, h0 + hi, :], rhs=vnew[:, h0 + hi, :], start=True, stop=True), inc=(hi == 3))
                    Sop("dve", [PB, "sc_c4" + pz + sfx], ["of" + sfx], lambda: nc.vector.tensor_tensor(out=G4(of), in0=p4(pbk), in1=bc4(sg(I_C4)), op=ALU.mult))
                    Sop("dve", [CB, "sc_c5" + pz + sfx], ["tA" + sfx], lambda: nc.vector.tensor_tensor(out=G4(tA), in0=p4(cbk), in1=bc4(sg(I_C5)), op=ALU.mult))
                    Sop("pool", ["of" + sfx, "tA" + sfx], ["of" + sfx], lambda: nc.gpsimd.tensor_tensor(out=G4(of), in0=G4(of), in1=G4(tA), op=ALU.add))
                    for hi in range(4):
                        Sop("pe", ["kdec" + sfx, "vnew" + sfx], [PB], lambda hi=hi: nc.tensor.matmul(
                            hbk(pbk, hi), lhsT=kdec[:, h0 + hi, :], rhs=vnew[:, h0 + hi, :], start=True, stop=True), inc=(hi == 3))
                    for hi in range(4):
                        Sop("dve", [PB, "Sf" + sfx, "sc_egl" + pz], ["Sf" + sfx], lambda hi=hi: nc.vector.scalar_tensor_tensor(
                            out=Sf[:, h0 + hi, :], in0=Sf[:, h0 + hi, :], scalar=sc[:, I_EGL, h0 + hi:h0 + hi + 1], in1=hbk(pbk, hi),
                            op0=ALU.mult, op1=ALU.add))
                    Sop("act", ["Sf" + sfx], ["Sbf" + sfx], lambda: nc.scalar.activation(out=G4(Sbf), in_=G4(Sf), func=AF.Copy))
                    ssov = sso[:, h0:h0 + 4]
                    Sop("act", ["of" + sfx], ["Ef" + sfx], lambda: nc.scalar.activation(out=G4(Ef), in_=G4(of), func=AF.Square))
                    Sop("dve", ["Ef" + sfx], ["sso" + sfx], lambda: nc.vector.tensor_reduce(out=ssov, in_=G4(Ef), axis=mybir.AxisListType.X, op=ALU.add))
                    Sop("dve", ["sso" + sfx], ["sso" + sfx], lambda: nc.vector.tensor_scalar(
                        out=ssov, in0=ssov, scalar1=1.0 / 128, scalar2=EPS, op0=ALU.mult, op1=ALU.add))
                    Sop("pool", ["sso" + sfx, "neghalf"], ["sso" + sfx], lambda: nc.gpsimd.tensor_tensor(out=ssov, in0=ssov, in1=neghalf[:, 0:4], op=ALU.pow))
                    Sop("dve", ["of" + sfx, "sso" + sfx], ["of" + sfx], lambda: nc.vector.tensor_tensor(out=G4(of), in0=G4(of), in1=bc4(ssov), op=ALU.mult))
                    Sop("dve", ["of" + sfx, "zs" + pz], ["ogb" + sfx], lambda: nc.vector.tensor_tensor(
                        out=ogb[:, h0 * 128:(h0 + 4) * 128], in0=G4(of).rearrange("p h d -> p (h d)"), in1=zs[:, h0 * 128:(h0 + 4) * 128], op=ALU.mult))
                    return ops, proj_end[0]

            def suffix_ops(ti):
                kind, idx = tiles[ti]
                par = ti % 2
                is_s = (kind == "s")
                xt = xts[par]
                xtn = "yt%d" % par
                ops = []

                def Sop(eng, reads, writes, fn, inc=True, dsem=None):
                    ops.append((eng, reads, writes, fn, inc, dsem))

                pT0 = psb(0)
                for k in range(8):
                    Sop("pe", ["ogb_0", "ogb_1", "identb"], ["ps0"], lambda k=k: nc.tensor.transpose(
                        out=pT0[:, k * 128:(k + 1) * 128], in_=ogb[:, k * 128:(k + 1) * 128], identity=identb[:]), inc=(k == 7))
                Sop("act", ["ps0"], ["ogT"], lambda: nc.scalar.activation(
                    out=ogT[:].rearrange("p k t -> p (k t)"), in_=pT0[:, 0:1024], func=AF.Copy))
                for nh in range(2):
                    for k in range(8):
                        Sop("pe", ["ogT", "wo1"], ["ps7"], lambda nh=nh, k=k: nc.tensor.matmul(
                            ps[:, 7, :], lhsT=ogT[:, k, :], rhs=wo1[:, k, nh * 512:(nh + 1) * 512],
                            start=(k == 0), stop=(k == 7)), inc=(k == 7))
                    Sop("dve", ["ps7", xtn], ["yo0"], lambda nh=nh, xt=xt, par=par: nc.vector.tensor_tensor(
                        out=yo[par][:, nh * 512:(nh + 1) * 512], in0=ps[:, 7, :], in1=xt[:, nh * 512:(nh + 1) * 512], op=ALU.add))
                Sop("act", ["yo0"], ["xn", "ss"], lambda par=par: nc.scalar.activation(out=junk[:], in_=yo[par][:], func=AF.Square, accum_out=ss[:, 0:1]))
                Sop("dve", ["ss"], ["ms"], lambda: nc.vector.tensor_scalar(
                    out=ms[:, 0:1], in0=ss[:, 0:1], scalar1=1.0 / D, scalar2=EPS, op0=ALU.mult, op1=ALU.add))
                Sop("pool", ["ms", "neghalf"], ["rstd"], lambda: nc.gpsimd.tensor_tensor(out=rstd[:, 0:1], in0=ms[:, 0:1], in1=neghalf[:, 0:1], op=ALU.pow))
                y = yo[par]
                yn = "yo0"
                Sop("dve", [yn, "rstd", "fgt"], [yn], lambda y=y: nc.vector.scalar_tensor_tensor(
                    out=y[:], in0=y[:], scalar=rstd[:, 0:1], in1=fgt[:], op0=ALU.mult, op1=ALU.mult))
                if is_s:
                    Sop("sp", [yn], ["ys_out"], lambda y=y, idx=idx: nc.sync.dma_start(out=ys[idx * 32:(idx + 1) * 32, :], in_=y[0:32, :]), dsem="st" + yn)
                else:
                    Sop("sp", [yn], ["yp_out"], lambda y=y, idx=idx: nc.sync.dma_start(out=yp[idx * 128:(idx + 1) * 128, :], in_=y[:]), dsem="st" + yn)
                return ops

            def emit_last(ti):
                kind, idx = tiles[ti]
                histf = histfs[ti % 2]
                pz = "p%d" % (ti % 2)
                is_s = (kind == "s")
                last = True
                if last:
                    dst = ssms[idx] if is_s else ssmp
                    S.op("sp", ["Sf_0", "Sf_1"], ["ssm_out"], lambda dst=dst: nc.sync.dma_start(out=dst.rearrange("h k v -> k h v"), in_=Sf[:]), dsem="o")
                    dstc = convs[idx] if is_s else convp
                    hv = Ef[0:3, :, :].rearrange("p h d -> p (h d)")
                    for rd in range(3):
                        for bl in range(8):
                            blk = rd * 8 + bl
                            S.op("pe", ["histf" + pz + "_0", "histf" + pz + "_1", "ident"], ["ps5", "ps6"], lambda blk=blk, bl=bl: nc.tensor.transpose(
                                out=ps[0:3, 5 + bl // 4, (bl % 4) * 128:(bl % 4 + 1) * 128], in_=histf[:, blk, :], identity=ident[:]), inc=(bl == 7))
                        S.op("dve", ["ps5", "ps6"], ["Ef_0", "Ef_1"], lambda hv=hv: nc.vector.tensor_copy(
                            out=hv, in_=ps[0:3, 5:7, :].rearrange("p b c -> p (b c)")))
                        S.op("sp", ["Ef_0", "Ef_1"], ["conv_out"], lambda dstc=dstc, rd=rd, hv=hv: nc.sync.dma_start(out=dstc[:, rd * 1024:(rd + 1) * 1024], in_=hv), dsem="o")

            def to_steps(ops_, maxpe=PESTEP):
                steps_ = []
                for o_ in ops_:
                    if steps_ and steps_[-1][-1][0] == o_[0] and o_[0] == "pe" and len(steps_[-1]) < maxpe:
                        steps_[-1].append(o_)
                    else:
                        steps_.append([o_])
                return steps_

            def emit_ops(ops_):
                for (eng_, r_, w_, fn_, inc_, ds_) in ops_:
                    S.op(eng_, r_, w_, fn_, inc=inc_, dsem=ds_)

            deferred = None
            prefix_done = set()
            PL1 = Planner()

            def emit_ops_p(ops_):
                for st_ in to_steps(ops_):
                    PL1.commit(st_)
                    emit_ops(st_)

            def front_ops(ti_, g_):
                o_, c_ = gstream(ti_, g_, front=True)
                return o_[:c_]

            for ti, (kind, idx) in enumerate(tiles):
                is_s = (kind == "s")
                last = is_s or idx == NT - 1
                if ti not in prefix_done:
                    emit_ops_p(prefix_ops(ti))
                    emit_ops_p(front_ops(ti, 0))
                    emit_ops_p(front_ops(ti, 1))
                opsA, cA = gstream(ti, 0)
                opsB, cB = gstream(ti, 1)
                stA, stB = to_steps(opsA[cA:]), to_steps(opsB[cB:])
                stX = to_steps(deferred) if deferred else []
                stX2 = []
                if ti + 1 < len(tiles) and OVERLAP:
                    opsD = prefix_ops(ti + 1)
                    prefix_done.add(ti + 1)
                    ldD = []
                    while opsD and opsD[0][0] == "sp":
                        ldD.append(opsD.pop(0))
                    stX = stX + ([ldD] if ldD else []) + to_steps(opsD)
                    gate_at = None
                    for si_, st_ in enumerate(stX):
                        if any("xnT" in o_[2] for o_ in st_):
                            gate_at = si_ + 1
                    assert gate_at is not None
                    stX2 = [[("gate", 2, gate_at)]] + to_steps(front_ops(ti + 1, 0)) + to_steps(front_ops(ti + 1, 1))
                if PLAN:
                    PL1.merge([stA, stB, stX, stX2], emit_ops)
                else:
                    for st_ in stA + stB + stX + stX2[1:]:
                        emit_ops(st_)
                if last:
                    emit_ops_p(suffix_ops(ti))
                    emit_last(ti)
                    deferred = None
                else:
                    if OVERLAP:
                        deferred = suffix_ops(ti)
                    else:
                        emit_ops_p(suffix_ops(ti))
                        deferred = None
            if deferred:
                emit_ops(deferred)
            S.barrier()

        S.finish()
    return nc


_CACHE = {}


def _get_program(NT, layers=2, debug=False):
    key = (NT, layers, debug)
    if key not in _CACHE:
        _CACHE[key] = build_program(NT, layers, debug)
    return _CACHE[key]


def make_in_maps(inputs, NT):
    f = lambda a: np.ascontiguousarray(np.asarray(a, dtype=np.float32))
    x_prompt = f(inputs["x_prompt"]); x_sample = f(inputs["x_sample"])
    cache_k = f(inputs["cache_k"]); cache_v = f(inputs["cache_v"])
    state_conv = f(inputs["state_conv"]); state_ssm = f(inputs["state_ssm"])
    norm_g = f(inputs["norm_g"])
    gcol = np.ascontiguousarray(norm_g.reshape(2, 8, 128).transpose(2, 0, 1))
    dngc = np.broadcast_to(f(inputs["dn_norm_g"])[0][:, None, None], (128, 1, 8))
    gcol = np.ascontiguousarray(np.concatenate([gcol, dngc], axis=1))
    fgb = np.ascontiguousarray(np.broadcast_to(f(inputs["final_norm_g"])[None, :], (128, D)))
    sinks = f(inputs["attn_sinks"])[0]
    sr = np.zeros((2, 4, 2, 64), np.float32)
    for half in range(2):
        for kv in range(4):
            for j in range(2):
                sr[half, kv, j, :] = sinks[4 * kv + 2 * j + half]
    sinkrow = np.concatenate([sr.reshape(1, 1024), sr[:, :, :, :32].reshape(1, 512)], axis=1)
    tabs_h, tabs2_h = _alibi_tables()
    cw = f(inputs["dn_conv_w"])[0]
    cwl = np.ascontiguousarray(cw.reshape(4, 24, 128).transpose(2, 1, 0)).reshape(128, 96)
    alogb = np.ascontiguousarray(np.broadcast_to(f(inputs["dn_a_log"])[0][None, :], (128, 8)))
    dtbb = np.ascontiguousarray(np.broadcast_to(f(inputs["dn_dt_bias"])[0][None, :], (128, 8)))
    dngb = np.ascontiguousarray(np.broadcast_to(f(inputs["dn_norm_g"])[0][None, :], (128, 128)))
    common = dict(
        gcol=gcol, fgb=fgb, w_in0=f(inputs["attn_w_in"])[0], w_out0=f(inputs["attn_w_out"])[0],
        w_in1=f(inputs["dn_w_in"])[0], w_out1=f(inputs["dn_w_out"])[0], sinkrow=sinkrow, cwl=cwl,
        alogb=alogb, dtbb=dtbb, dngb=dngb, ident=np.eye(128, dtype=np.float32),
        tabs=tabs_h, tabs2=tabs2_h, masks=_masks_l1())
    maps = []
    for c in range(NCORES):
        sc = state_conv[0, 4 * c:4 * c + 4]
        sconv = np.ascontiguousarray(sc.reshape(4, 3, 24, 128).transpose(3, 2, 0, 1))
        m = dict(common)
        m.update(
            xp=np.ascontiguousarray(x_prompt[c, :NT * 128]),
            xs=np.ascontiguousarray(x_sample[4 * c:4 * c + 4].reshape(128, D)),
            ck=np.ascontiguousarray(cache_k[0, 4 * c:4 * c + 4].reshape(4, 128, 256)),
            cv=np.ascontiguousarray(cache_v[0, 4 * c:4 * c + 4].reshape(4, 128, 256)),
            sconv=sconv, sssm=np.ascontiguousarray(state_ssm[0, 4 * c:4 * c + 4]))
        maps.append(m)
    return maps


def run(inputs, NT=NT_FULL, layers=2, debug=False, trace=False):
    nc = _get_program(NT, layers, debug)
    in_maps = make_in_maps(inputs, NT)
    res = run_bass_kernel_spmd(nc, in_maps, core_ids=list(range(NCORES)))
    return res


def kernel(**inputs):
    NT = NT_FULL
    res = run(inputs, NT)
    R = res.results
    st = lambda name: np.stack([np.asarray(R[c][name]) for c in range(NCORES)])
    y_prompt = st("yp").reshape(NCORES, NT * 128, D)
    y_sample = st("ys").reshape(NCORES * 4, 32, D)
    kwp = st("kwp").reshape(1, NCORES, 128, 4, 64)
    vwp = st("vwp").reshape(1, NCORES, 128, 4, 64)
    convp = st("convp").reshape(1, NCORES, 3, 3072)
    ssmp = st("ssmp").reshape(1, NCORES, 8, 128, 128)
    kws = st("kws").reshape(1, NCORES * 4, 128, 4, 64)
    vws = st("vws").reshape(1, NCORES * 4, 128, 4, 64)
    convs = st("convs").reshape(1, NCORES * 4, 3, 3072)
    ssms = st("ssms").reshape(1, NCORES * 4, 8, 128, 128)
    return (y_prompt, y_sample, kwp, vwp, convp, ssmp, kws, vws, convs, ssms)
```

```python
import numpy as np
import os as _os_
from contextlib import ExitStack
import concourse.bass as bass
import concourse.mybir as mybir
from concourse.bass_utils import run_bass_kernel_spmd

F32 = mybir.dt.float32
BF16 = mybir.dt.bfloat16
AF = mybir.ActivationFunctionType
ALU = mybir.AluOpType

NCORES = 8
D = 1024
EPS = 1e-6
NT_FULL = 32
SAME_ENG_SYNC = (_os_.environ.get('K_SES', '1') == '1')
import os as _os
STOP = int(_os.environ.get('K_STOP', '0'))
ASTOP = int(_os.environ.get('K_ASTOP', '99'))
GSTAG = int(_os.environ.get('K_GSTAG', '12'))
OVERLAP = (_os.environ.get('K_OVERLAP', '1') == '1')
OVERLAP0 = (_os.environ.get('K_OVERLAP0', '1') == '1')
BSTEPS = int(_os.environ.get('K_BSTEPS', '3'))
AUXSTEPS = int(_os.environ.get('K_AUXSTEPS', '1'))
PLAN = (_os.environ.get('K_PLAN', '1') == '1')
PESTEP = int(_os.environ.get('K_PESTEP', '8'))
SKIP = set(_os.environ.get('K_SKIP', '').split(','))


class Sched:
    def __init__(self, nc, es):
        self.nc = nc
        self.es = es
        self.eng = {"pe": nc.tensor, "act": nc.scalar, "dve": nc.vector, "pool": nc.gpsimd, "sp": nc.sync}
        self.sem = {}
        self.cnt = {}
        for e in ["pe", "act", "dve", "pool"]:
            self.sem[e] = es.enter_context(nc.semaphore("s_" + e))
            self.cnt[e] = 0
        self.seen = {e: {} for e in self.eng}
        self.lastw = {}
        self.readers = {}
        self.pending = {e: 0 for e in self.eng}
        self.rec = None

    def dma_sem(self, key):
        if key not in self.sem:
            self.sem[key] = self.es.enter_context(self.nc.semaphore("d_" + key))
            self.cnt[key] = 0
        return key

    def _wait(self, eng, key, val):
        if val <= 0 or self.seen[eng].get(key, 0) >= val:
            return
        self.eng[eng].wait_ge(self.sem[key], val)
        self.seen[eng][key] = val

    def op(self, eng, reads, writes, fn, inc=True, dsem=None):
        if self.rec is not None:
            self.rec.append((eng, list(reads), list(writes), fn, inc, dsem))
            return None
        writes = list(writes) + [r for r in reads if r.startswith("ps") and r not in writes]
        deps = {}

        def add(tok):
            k, v = tok
            if deps.get(k, 0) < v:
                deps[k] = v

        for r in reads:
            if r in self.lastw:
                add(self.lastw[r])
        for w in writes:
            if w in self.lastw:
                add(self.lastw[w])
            for k, v in self.readers.get(w, {}).items():
                add((k, v))
        if dsem is not None:
            self.dma_sem(dsem)
            add((dsem, self.cnt[dsem]))
        for k, v in deps.items():
            if k == eng and (eng == "pe" or not SAME_ENG_SYNC):
                continue
            self._wait(eng, k, v)
        inst = fn()
        if dsem is not None:
            self.cnt[dsem] += 16
            inst.then_inc(self.sem[dsem], 16)
            tok = (dsem, self.cnt[dsem])
        elif inc:
            self.cnt[eng] += 1
            inst.then_inc(self.sem[eng], 1)
            tok = (eng, self.cnt[eng])
        else:
            tok = (eng, self.cnt[eng] + 1)
        for w in writes:
            self.lastw[w] = tok
            self.readers[w] = {}
        for r in reads:
            d = self.readers.setdefault(r, {})
            if d.get(tok[0], 0) < tok[1]:
                d[tok[0]] = tok[1]
        return tok

    def barrier(self):
        for e in self.eng:
            for k in list(self.sem.keys()):
                if k == e and e == "pe":
                    continue
                self._wait(e, k, self.cnt[k])

    def finish(self):
        for k in list(self.sem.keys()):
            self._wait("sp", k, self.cnt[k])


NEG = -30000.0


_SMALL = ("sc_", "ss", "ms", "rstd", "sso", "r16", "rec_s")


def _op_cost(op):
    eng, reads, writes = op[0], op[1], op[2]
    if eng == "pe":
        big = any(r in ("wz", "wo1", "wo", "wkv") for r in reads)
        return 0.26 if big else 0.09
    if eng == "sp":
        return 0.05
    small = any(w.startswith(_SMALL) for w in writes)
    if eng == "pool":
        return 0.9 if small else 1.15
    if small:
        return 0.15
    return 0.68


class Planner:
    def __init__(self):
        self.free = {}
        self.wr = {}
        self.rd = {}

    def est(self, step):
        eng = step[0][0]
        start = self.free.get(eng, 0.0)
        for op in step:
            for r in op[1]:
                start = max(start, self.wr.get(r, 0.0))
                if r.startswith("ps"):
                    start = max(start, self.rd.get(r, 0.0))
            for w in op[2]:
                start = max(start, self.wr.get(w, 0.0), self.rd.get(w, 0.0))
        return start

    def commit(self, step):
        eng = step[0][0]
        t = self.est(step)
        for op in step:
            t += _op_cost(op)
        self.free[eng] = t
        lat = 2.0 if eng == "sp" else 0.08
        for op in step:
            for w in op[2]:
                self.wr[w] = t + lat
            for r in op[1]:
                if self.rd.get(r, 0.0) < t + lat:
                    self.rd[r] = t + lat
        return t

    def merge(self, streams, emit):
        ptr = [0] * len(streams)
        while True:
            best, bs = None, None
            for i, st in enumerate(streams):
                while ptr[i] < len(st) and st[ptr[i]][0][0] == "gate" and ptr[st[ptr[i]][0][1]] >= st[ptr[i]][0][2]:
                    ptr[i] += 1
                if ptr[i] < len(st) and st[ptr[i]][0][0] != "gate":
                    e = self.est(st[ptr[i]])
                    if best is None or e < bs - 1e-9:
                        best, bs = i, e
            if best is None:
                assert all(ptr[i] >= len(st) for i, st in enumerate(streams)), "planner deadlock"
                break
            step = streams[best][ptr[best]]
            self.commit(step)
            emit(step)
            ptr[best] += 1


def _alibi_tables():
    slopes = 2.0 ** (-8.0 * np.arange(1, 17, dtype=np.float64) / 16.0)
    tabs = np.full((128, 8, 512), NEG, np.float32)
    q = np.arange(64)[None, :]
    s = np.arange(128)[:, None]
    for c in range(2):
        for blk in range(2):
            for half in range(2):
                ti = c * 4 + blk * 2 + half
                if c == 0 and blk == 0:
                    dist = q + 128 - s; valid = np.ones((128, 1), bool)
                elif c == 0 and blk == 1:
                    dist = np.abs(q - s); valid = s < 64
                elif c == 1 and blk == 0:
                    dist = np.abs(64 + q - s); valid = np.ones((128, 1), bool)
                else:
                    dist = 192 + q - s; valid = s >= 64
                for kv in range(4):
                    for j in range(2):
                        h = 4 * kv + 2 * j + half
                        v = np.where(valid, -slopes[h] * dist, NEG)
                        tabs[:, ti, (kv * 2 + j) * 64:(kv * 2 + j + 1) * 64] = v
    tabs2 = np.full((128, 10, 256), NEG, np.float32)
    t = np.arange(32)[None, :]
    for half in range(2):
        for kv in range(4):
            for j in range(2):
                h = 4 * kv + 2 * j + half
                cs = slice((kv * 2 + j) * 32, (kv * 2 + j + 1) * 32)
                tabs2[:, half, cs] = -slopes[h] * (t + 128 - s)
                for b in range(4):
                    sp = s - 32 * b
                    valid = (sp >= 0) & (sp < 32)
                    tabs2[:, 2 + 2 * b + half, cs] = np.where(valid, -slopes[h] * np.abs(t - sp), NEG)
    return tabs, tabs2


def _masks():
    p = np.arange(128)[:, None]
    f = np.arange(128)[None, :]
    same = (p // 32) == (f // 32)
    m = np.zeros((128, 10, 128), np.float32)
    m[:, 0] = (p <= f)
    m[:, 1] = (p > f)
    m[:, 2] = (f > p)
    m[:, 3] = (f >= p)
    m[:, 4] = 1.0
    m[:, 5] = (p <= f) & same
    m[:, 6] = (p > f) & same
    m[:, 7] = (f > p) & same
    m[:, 8] = (f >= p) & same
    m[:, 9] = same
    return m


def _masks_l1():
    m = _masks()
    p = np.arange(128)[:, None]
    f = np.arange(128)[None, :]
    same32 = (p // 32) == (f // 32)
    same64 = (p // 64) == (f // 64)
    off64 = (same64 & ~same32).astype(np.float32)
    off128 = (~same64).astype(np.float32)
    return np.ascontiguousarray(np.concatenate([m[:, [0, 1, 2, 3, 9]], off64[:, None, :], off128[:, None, :]], axis=1))


def build_program(NT, layers=2, debug=False):
    nc = bass.Bass("TRN2", target_bir_lowering=False)

    def din(name, shape):
        return nc.dram_tensor(name, list(shape), F32, kind="ExternalInput").ap()

    def dout(name, shape):
        return nc.dram_tensor(name, list(shape), F32, kind="ExternalOutput").ap()

    xp = din("xp", [NT * 128, D])
    xs = din("xs", [128, D])
    ck = din("ck", [4, 128, 256])
    cv = din("cv", [4, 128, 256])
    sconv = din("sconv", [128, 24, 4, 3])
    sssm = din("sssm", [4, 8, 128, 128])
    gcol = din("gcol", [128, 3, 8])
    fgb = din("fgb", [128, D])
    w_in0 = din("w_in0", [D, 2560])
    w_out0 = din("w_out0", [D, D])
    w_in1 = din("w_in1", [D, 4112])
    w_out1 = din("w_out1", [D, D])
    sinkrow = din("sinkrow", [1, 1536])
    cwl = din("cwl", [128, 96])
    alogb = din("alogb", [128, 8])
    dtbb = din("dtbb", [128, 8])
    dngb = din("dngb", [128, 128])
    identd = din("ident", [128, 128])
    tabsd = din("tabs", [128, 8, 512])
    tabs2d = din("tabs2", [128, 10, 256])
    masksd = din("masks", [128, 7, 128])

    yp = dout("yp", [NT * 128, D])
    ys = dout("ys", [128, D])
    kwp = dout("kwp", [128, 256])
    vwp = dout("vwp", [128, 256])
    convp = dout("convp", [3, 3072])
    ssmp = dout("ssmp", [8, 128, 128])
    kws = dout("kws", [4, 128, 256])
    vws = dout("vws", [4, 128, 256])
    convs = dout("convs", [4, 3, 3072])
    ssms = dout("ssms", [4, 8, 128, 128])
    if debug:
        x1d = dout("x1d", [(NT + 1) * 128, D])
    else:
        x1d = nc.dram_tensor("x1d", [(NT + 1) * 128, D], F32, kind="Internal").ap()

    with ExitStack() as es:
        S = Sched(nc, es)

        def sb(st, name, shape, dt):
            return st.enter_context(nc.sbuf_tensor("sb_" + name, list(shape), dt))

        ps = es.enter_context(nc.psum_tensor("ps", [128, 8, 512], F32))

        def psb(bank):
            return ps[:, bank, :].bitcast(BF16)

        ident = sb(es, "ident", [128, 128], F32)
        identb = sb(es, "identb", [128, 128], BF16)
        onesb = sb(es, "onesb", [128, 128], BF16)
        neghalf = sb(es, "neghalf", [128, 16], F32)
        gcs = sb(es, "gcs", [128, 3, 8], F32)
        S.op("sp", [], ["ident"], lambda: nc.sync.dma_start(out=ident[:], in_=identd), dsem="c")
        S.op("sp", [], ["gcs"], lambda: nc.sync.dma_start(out=gcs[:], in_=gcol), dsem="c")
        S.op("dve", ["ident"], ["identb"], lambda: nc.vector.tensor_copy(out=identb[:], in_=ident[:]))
        S.op("dve", [], ["onesb"], lambda: nc.vector.memset(onesb[:], 1.0))
        S.op("dve", [], ["neghalf"], lambda: nc.vector.memset(neghalf[:], -0.5))

        def load_weight(st_w, src, ncols, dst_fn, lidx, tag):
            srcv = src.rearrange("(k p) n -> p k n", p=128)
            nch = (ncols + 511) // 512
            for ci in range(nch):
                c0 = ci * 512
                c1 = min(ncols, c0 + 512)
                stg = st_w[ci % 2]
                nm = "stg%d" % (ci % 2)
                S.op("sp", [], [nm], lambda stg=stg, c0=c0, c1=c1: nc.sync.dma_start(
                    out=stg[:, :, 0:c1 - c0], in_=srcv[:, :, c0:c1]), dsem="w" + nm)
                dst_fn(c0, c1, stg, nm, "dve" if ci % 2 == 0 else "pool")

        def scaled_cast(eng, dst, stg, nm, w, lidx, dname):
            if eng != "dve":
                if lidx is None:
                    S.op("act", [nm], [dname], lambda: nc.scalar.activation(out=dst, in_=stg[:, :, 0:w], func=AF.Copy))
                else:
                    for k in range(8):
                        S.op("act", [nm, "gcs"], [dname], lambda k=k: nc.scalar.activation(
                            out=dst[:, k, :], in_=stg[:, k, 0:w], func=AF.Copy, scale=gcs[:, lidx, k:k + 1]))
                return
            e = nc.vector
            if lidx is None:
                S.op(eng, [nm], [dname], lambda: e.tensor_copy(out=dst, in_=stg[:, :, 0:w]))
            else:
                S.op(eng, [nm, "gcs"], [dname], lambda: e.tensor_tensor(
                    out=dst, in0=stg[:, :, 0:w],
                    in1=gcs[:, lidx, :].unsqueeze(2).broadcast_to([128, 8, w]), op=ALU.mult))

        def rms_and_transpose(xt, xtn, ssn, xn, xnT, st, jn="junk", emit=None):
            emit = emit or S.op
            junk, ss, ms, rstd = st
            emit("act", [xtn], (jn if isinstance(jn, list) else [jn]) + ["ss"], lambda: nc.scalar.activation(
                out=junk[:], in_=xt[:], func=AF.Square, accum_out=ss[:, 0:1]))
            emit("dve", ["ss"], ["ms"], lambda: nc.vector.tensor_scalar(
                out=ms[:, 0:1], in0=ss[:, 0:1], scalar1=1.0 / D, scalar2=EPS, op0=ALU.mult, op1=ALU.add))
            emit("pool", ["ms", "neghalf"], ["rstd"], lambda: nc.gpsimd.tensor_tensor(
                out=rstd[:, 0:1], in0=ms[:, 0:1], in1=neghalf[:, 0:1], op=ALU.pow))
            emit("dve", [xtn, "rstd"], ["xn"], lambda: nc.vector.tensor_scalar(
                out=xn[:], in0=xt[:], scalar1=rstd[:, 0:1], scalar2=None, op0=ALU.mult))
            pT = psb(0)
            for k in range(8):
                emit("pe", ["xn", "identb"], ["ps0"], lambda k=k: nc.tensor.transpose(
                    out=pT[:, k * 128:(k + 1) * 128], in_=xn[:, k * 128:(k + 1) * 128], identity=identb[:]),
                    inc=(k == 7))
            emit("act", ["ps0"], ["xnT"], lambda: nc.scalar.activation(
                out=xnT[:].rearrange("p k t -> p (k t)"), in_=pT[:, 0:1024], func=AF.Copy))

        with ExitStack() as e0:
            wq = sb(e0, "wq", [128, 8, 1024], BF16)
            wkd = sb(e0, "wkd", [128, 8, 512], BF16)
            wkv = sb(e0, "wkv", [128, 8, 512], BF16)
            wg = sb(e0, "wg", [128, 8, 1024], BF16)
            wo = sb(e0, "wo", [128, 8, 1024], BF16)
            tabs = sb(e0, "tabs", [128, 8, 512], F32)
            esink = sb(e0, "esink", [128, 1536], BF16)
            esf = sb(e0, "esf", [1, 1536], F32)
            tabs2 = sb(e0, "tabs2", [128, 10, 256], F32)
            S.op("sp", [], ["tabs"], lambda: nc.sync.dma_start(out=tabs2[:], in_=tabs2d), dsem="c")
            S.op("dve", [], ["esink"], lambda: nc.vector.memset(esink[:], 0.0))
            S.op("sp", [], ["tabs"], lambda: nc.sync.dma_start(out=tabs[:], in_=tabsd), dsem="c")
            S.op("sp", [], ["esf"], lambda: nc.sync.dma_start(out=esf[:], in_=sinkrow), dsem="c")
            S.op("act", ["esf"], ["esink"], lambda: nc.scalar.activation(out=esink[0:1, :], in_=esf[:], func=AF.Exp))

            with ExitStack() as ew:
                stg = [sb(ew, "stg0", [128, 8, 512], F32), sb(ew, "stg1", [128, 8, 512], F32)]

                def dst0(c0, c1, st, nm, eng):
                    if c0 < 1024:
                        scaled_cast(eng, wq[:, :, c0:c1], st, nm, 512, 0, "wq")
                    elif c0 == 1024:
                        scaled_cast(eng, wkv[:, :, :], st, nm, 512, 0, "wkv")
                        e = nc.vector if eng == "dve" else nc.gpsimd
                        wkd5 = wkd[:].rearrange("p k (v r d) -> p k v r d", v=4, r=2)
                        for r in range(2):
                            S.op(eng, [nm, "gcs"], ["wkd"], lambda r=r: e.tensor_tensor(
                                out=wkd5[:, :, :, r, :],
                                in0=st[:, :, 0:256].rearrange("p k (v d) -> p k v d", v=4),
                                in1=gcs[:, 0, :].unsqueeze(2).unsqueeze(3).broadcast_to([128, 8, 4, 64]),
                                op=ALU.mult))
                    else:
                        scaled_cast(eng, wg[:, :, c0 - 1536:c1 - 1536], st, nm, 512, 0, "wg")

                load_weight(stg, w_in0, 2560, dst0, 0, "wi0")
                load_weight(stg, w_out0, 1024,
                            lambda c0, c1, st, nm, eng: scaled_cast(eng, wo[:, :, c0:c1], st, nm, 512, None, "wo"),
                            None, "wo0")
                S.barrier()

            if layers >= 1:
                xts = [sb(e0, "xt%d" % i, [128, D], F32) for i in range(3)]
                x1t = [sb(e0, "x1t0", [128, D], F32), sb(e0, "x1t1", [128, D], F32)]
                junk = sb(e0, "junk", [128, D], BF16)
                ss = sb(e0, "ss", [128, 1], F32)
                ms = sb(e0, "ms", [128, 1], F32)
                rstd = sb(e0, "rstd", [128, 1], F32)
                xn = sb(e0, "xn", [128, D], BF16)
                xnT = sb(e0, "xnT", [128, 8, 128], BF16)
                qTs = [sb(e0, "qT%d" % i, [128, 8, 128], BF16) for i in range(2)]
                gTs = [sb(e0, "gT%d" % i, [128, 8, 128], BF16) for i in range(2)]
                ogT = sb(e0, "ogT", [128, 8, 128], BF16)
                kT = [sb(e0, "kT%d" % i, [128, 4, 128], BF16) for i in range(3)]
                vtok = [sb(e0, "vtok%d" % i, [128, 4, 64], BF16) for i in range(3)]
                kvout = sb(e0, "kvout", [128, 512], F32)
                tmp = [sb(e0, "tmp%d" % i, [128, 256], F32) for i in range(4)]
                pT_ = [sb(e0, "pT%d" % i, [128, 256], BF16) for i in range(8)]
                recs = [sb(e0, "rec%d" % i, [128, 256], F32) for i in range(2)]
                t1s = [sb(e0, "t1_%d" % i, [128, 256], F32) for i in range(2)]
                CS = [dict(A=2, B=3, OD=4, tmp=[0, 1], pT=[0, 1, 2, 3], ctr=[0], rec=recs[0], recn="rec0", t1=t1s[0], t1n="t1_0"),
                      dict(A=5, B=6, OD=7, tmp=[2, 3], pT=[4, 5, 6, 7], ctr=[0], rec=recs[1], recn="rec1", t1=t1s[1], t1n="t1_1")]
                ckf = [sb(e0, "ckf%d" % b, [128, 256], F32) for b in range(4)]
                cvf = [sb(e0, "cvf%d" % b, [128, 256], F32) for b in range(4)]
                ckdup = sb(e0, "ckdup", [128, 4, 128], BF16)
                kTc = [sb(e0, "kTc%d" % b, [128, 4, 128], BF16) for b in range(4)]
                cvb = [sb(e0, "cvb%d" % b, [128, 4, 64], BF16) for b in range(4)]
                tmpctr = [0]

                def attend(q0, QW, blocks, qT, qTn, gT, gTn, cs):
                    W2 = 4 * QW
                    nb = len(blocks)
                    OD = cs["OD"]
                    ODn = "ps%d" % OD
                    for kvg in range(2):
                        banks = (cs["A"], cs["B"])
                        pts = {}
                        for bi, blk in enumerate(blocks):
                            for kvi in range(2):
                                kv = 2 * kvg + kvi
                                for half in range(2):
                                    bank = banks[half]
                                    out = ps[:, bank, bi * W2 + kvi * 2 * QW: bi * W2 + (kvi + 1) * 2 * QW]
                                    S.op("pe", [blk["kn"], qTn], ["ps%d" % bank], lambda out=out, blk=blk, kv=kv, half=half, qT=qT: nc.tensor.matmul(
                                        out.rearrange("p (j q) -> p j q", j=2),
                                        lhsT=blk["kT"][half * 64:(half + 1) * 64, kv, :],
                                        rhs=qT[half * 64:(half + 1) * 64, 2 * kv:2 * kv + 2, q0:q0 + QW],
                                        start=True, stop=True), inc=(kvi == 1))
                            for half in range(2):
                                bank = banks[half]
                                tt, ti0 = blk["tab"]
                                ci = cs["ctr"][0]
                                cs["ctr"][0] += 1
                                tmi = cs["tmp"][ci % len(cs["tmp"])]
                                pti = cs["pT"][ci % len(cs["pT"])]
                                tm, tmn = tmp[tmi], "tmp%d" % tmi
                                pt, ptn = pT_[pti], "pT%d" % pti
                                tabv = tt[:, ti0 + half, kvg * W2:(kvg + 1) * W2]
                                S.op("dve", ["ps%d" % bank, "tabs"], [tmn], lambda tm=tm, bank=bank, bi=bi, tabv=tabv: nc.vector.tensor_tensor(
                                    out=tm[:, 0:W2], in0=ps[:, bank, bi * W2:(bi + 1) * W2], in1=tabv, op=ALU.add))
                                S.op("act", [tmn], [ptn], lambda tm=tm, pt=pt: nc.scalar.activation(
                                    out=pt[:, 0:W2], in_=tm[:, 0:W2], func=AF.Exp))
                                pts[(bi, half)] = (pt, ptn)
                        for kvi in range(2):
                            kv = 2 * kvg + kvi
                            for half in range(2):
                                o_out = ps[half * 64:(half + 1) * 64, OD, kvi * 2 * QW:(kvi + 1) * 2 * QW]
                                d_out = ps[half * 64:(half + 1) * 64, OD, 256 + kvi * 2 * QW:256 + (kvi + 1) * 2 * QW]
                                for bi, blk in enumerate(blocks):
                                    pt, ptn = pts[(bi, half)]
                                    S.op("pe", [ptn, blk["vn"]], [ODn], lambda o_out=o_out, blk=blk, kv=kv, pt=pt, bi=bi, kvi=kvi: nc.tensor.matmul(
                                        o_out, lhsT=blk["v"][:, kv, :], rhs=pt[:, kvi * 2 * QW:(kvi + 1) * 2 * QW],
                                        start=(bi == 0), stop=(bi == nb - 1)), inc=False)
                                for bi, blk in enumerate(blocks):
                                    pt, ptn = pts[(bi, half)]
                                    S.op("pe", [ptn, "onesb"], [ODn], lambda d_out=d_out, pt=pt, bi=bi, kvi=kvi: nc.tensor.matmul(
                                        d_out, lhsT=onesb[:, 0:64], rhs=pt[:, kvi * 2 * QW:(kvi + 1) * 2 * QW],
                                        start=(bi == 0), stop=False), inc=False)
                                e0c = (0 if QW == 64 else 1024) + half * 8 * QW + kv * 2 * QW
                                S.op("pe", ["esink", "onesb"], [ODn], lambda d_out=d_out, e0c=e0c: nc.tensor.matmul(
                                    d_out, lhsT=onesb[:, 0:64], rhs=esink[:, e0c:e0c + 2 * QW],
                                    start=False, stop=True), inc=True)
                        rec, recn, t1, t1n = cs["rec"], cs["recn"], cs["t1"], cs["t1n"]
                        S.op("dve", [ODn], [recn], lambda rec=rec: nc.vector.reciprocal(out=rec[:, 0:W2], in_=ps[:, OD, 256:256 + W2]))
                        S.op("dve", [ODn, recn], [t1n], lambda rec=rec, t1=t1: nc.vector.tensor_tensor(
                            out=t1[:, 0:W2], in0=ps[:, OD, 0:W2], in1=rec[:, 0:W2], op=ALU.mult))
                        S.op("pool", [t1n, gTn], ["ogT%d_%d" % (kvg, q0)], lambda t1=t1, gT=gT, kvg=kvg: nc.gpsimd.tensor_tensor(
                            out=ogT[:, 4 * kvg:4 * kvg + 4, q0:q0 + QW], in0=t1[:, 0:W2].rearrange("p (a q) -> p a q", a=4),
                            in1=gT[:, 4 * kvg:4 * kvg + 4, q0:q0 + QW], op=ALU.mult))

                def l0_load(t):
                    if t > NT:
                        return
                    is_s = (t == NT)
                    p3 = t % 3
                    xt = xts[p3]
                    xtn = "xt%d" % p3
                    src = xs if is_s else xp[t * 128:(t + 1) * 128, :]
                    S.op("sp", [], [xtn], lambda xt=xt, src=src: nc.sync.dma_start(out=xt[:], in_=src), dsem="ld" + xtn)
                    if is_s:
                        for b in range(4):
                            S.op("sp", [], ["ckf%d" % b], lambda b=b: nc.sync.dma_start(out=ckf[b][:], in_=ck[b]), dsem="cache")
                            S.op("sp", [], ["cvf%d" % b], lambda b=b: nc.sync.dma_start(out=cvf[b][:], in_=cv[b]), dsem="cache")

                def l0_front(t):
                    is_s = (t == NT)
                    par = t % 2
                    p3 = t % 3
                    pv3 = (t - 1) % 3
                    xt = xts[p3]
                    xtn = "xt%d" % p3
                    qT, qTn = qTs[par], "qT%d" % par
                    gT, gTn = gTs[par], "gT%d" % par
                    kTn = "kT%d" % p3
                    vn = "vtok%d" % p3
                    rms_and_transpose(xt, xtn, "ss", xn, xnT, (junk, ss, ms, rstd))
                    if STOP == 1:
                        return
                    for (wsrc, wn, dst, dn, func, scale) in ((wq, "wq", qT, qTn, AF.Copy, 0.125), (wg, "wg", gT, gTn, AF.Silu, 1.0)):
                        for hb in range(2):
                            bank = 1 - hb
                            for bi in range(4):
                                blk = hb * 4 + bi
                                for k in range(8):
                                    S.op("pe", ["xnT", wn], ["ps%d" % bank], lambda bank=bank, bi=bi, blk=blk, k=k, wsrc=wsrc: nc.tensor.matmul(
                                        ps[:, bank, bi * 128:(bi + 1) * 128], lhsT=wsrc[:, k, blk * 128:(blk + 1) * 128],
                                        rhs=xnT[:, k, :], start=(k == 0), stop=(k == 7)), inc=(k == 7 and bi == 3))
                            S.op("act", ["ps%d" % bank], [dn], lambda bank=bank, hb=hb, dst=dst, func=func, scale=scale: nc.scalar.activation(
                                out=dst[:, hb * 4:(hb + 1) * 4, :].rearrange("p a t -> p (a t)"), in_=ps[:, bank, :], func=func, scale=scale))
                    if STOP == 2:
                        return
                    for kv in range(4):
                        for k in range(8):
                            S.op("pe", ["xnT", "wkd"], ["ps1"], lambda kv=kv, k=k: nc.tensor.matmul(
                                ps[:, 1, kv * 128:(kv + 1) * 128], lhsT=wkd[:, k, kv * 128:(kv + 1) * 128],
                                rhs=xnT[:, k, :], start=(k == 0), stop=(k == 7)), inc=(k == 7 and kv == 3))
                    S.op("act", ["ps1"], [kTn], lambda p3=p3: nc.scalar.activation(
                        out=kT[p3][:].rearrange("p a t -> p (a t)"), in_=ps[:, 1, :], func=AF.Copy))
                    if STOP == 3:
                        return
                    for k in range(8):
                        S.op("pe", ["xnT", "wkv"], ["ps0"], lambda k=k: nc.tensor.matmul(
                            ps[:, 0, :], lhsT=xnT[:, k, :], rhs=wkv[:, k, :], start=(k == 0), stop=(k == 7)), inc=(k == 7))
                    S.op("dve", ["ps0"], [vn], lambda p3=p3: nc.vector.tensor_copy(
                        out=vtok[p3][:].rearrange("p a d -> p (a d)"), in_=ps[:, 0, 256:512]))
                    if t >= NT - 1 and "kvout" not in SKIP:
                        S.op("act", ["ps0"], ["kvout"], lambda: nc.scalar.activation(out=kvout[:], in_=ps[:, 0, :], func=AF.Copy))
                    if t == NT - 1 and "kwp" not in SKIP:
                        S.op("sp", ["kvout"], ["kwp"], lambda: nc.sync.dma_start(out=kwp, in_=kvout[:, 0:256]), dsem="o")
                        S.op("sp", ["kvout"], ["vwp"], lambda: nc.sync.dma_start(out=vwp, in_=kvout[:, 256:512]), dsem="o")

                def l0_pre(t):
                    is_s = (t == NT)
                    par = t % 2
                    p3 = t % 3
                    pv3 = (t - 1) % 3
                    xt = xts[p3]
                    xtn = "xt%d" % p3
                    qT, qTn = qTs[par], "qT%d" % par
                    gT, gTn = gTs[par], "gT%d" % par
                    kTn = "kT%d" % p3
                    vn = "vtok%d" % p3
                    if is_s:
                        for b in range(4):
                            ckd5 = ckdup[:].rearrange("p v (r d) -> p v r d", r=2)
                            for r in range(2):
                                S.op("dve", ["ckf%d" % b], ["ckdup"], lambda b=b, r=r: nc.vector.tensor_copy(
                                    out=ckd5[:, :, r, :], in_=ckf[b][:].rearrange("p (v d) -> p v d", v=4)))
                            pT = psb(2)
                            for kv in range(4):
                                S.op("pe", ["ckdup", "identb"], ["ps2"], lambda kv=kv, pT=pT: nc.tensor.transpose(
                                    out=pT[:, kv * 128:(kv + 1) * 128], in_=ckdup[:, kv, :], identity=identb[:]), inc=(kv == 3))
                            S.op("act", ["ps2"], ["kTc%d" % b], lambda b=b, pT=pT: nc.scalar.activation(
                                out=kTc[b][:].rearrange("p a t -> p (a t)"), in_=pT[:, 0:512], func=AF.Copy))
                            S.op("dve", ["cvf%d" % b], ["cvb%d" % b], lambda b=b: nc.vector.tensor_copy(
                                out=cvb[b][:].rearrange("p a d -> p (a d)"), in_=cvf[b][:]))
                            S.op("sp", ["ckf%d" % b], ["kws"], lambda b=b: nc.sync.dma_start(out=kws[b, 0:96, :], in_=ckf[b][32:128, :]), dsem="o")
                            S.op("sp", ["cvf%d" % b], ["vws"], lambda b=b: nc.sync.dma_start(out=vws[b, 0:96, :], in_=cvf[b][32:128, :]), dsem="o")
                            S.op("sp", ["kvout"], ["kws"], lambda b=b: nc.sync.dma_start(out=kws[b, 96:128, :], in_=kvout[b * 32:(b + 1) * 32, 0:256]), dsem="o")
                            S.op("sp", ["kvout"], ["vws"], lambda b=b: nc.sync.dma_start(out=vws[b, 96:128, :], in_=kvout[b * 32:(b + 1) * 32, 256:512]), dsem="o")

                def l0_chunk(t, c):
                    is_s = (t == NT)
                    par = t % 2
                    p3 = t % 3
                    pv3 = (t - 1) % 3
                    xt = xts[p3]
                    xtn = "xt%d" % p3
                    qT, qTn = qTs[par], "qT%d" % par
                    gT, gTn = gTs[par], "gT%d" % par
                    kTn = "kT%d" % p3
                    vn = "vtok%d" % p3
                    cs = CS[c]
                    if not is_s:
                        blocks = []
                        cur = dict(kT=kT[p3], kn=kTn, v=vtok[p3], vn=vn)
                        prv = dict(kT=kT[pv3], kn="kT%d" % pv3, v=vtok[pv3], vn="vtok%d" % pv3)
                        if c == 0:
                            if t > 0:
                                blocks.append(dict(prv, tab=(tabs, 0)))
                            blocks.append(dict(cur, tab=(tabs, 2)))
                        else:
                            blocks.append(dict(cur, tab=(tabs, 4)))
                            if t > 0:
                                blocks.append(dict(prv, tab=(tabs, 6)))
                        attend(c * 64, 64, blocks, qT, qTn, gT, gTn, cs)
                    else:
                        for b in (c, c + 2):
                            blocks = [dict(kT=kTc[b], kn="kTc%d" % b, v=cvb[b], vn="cvb%d" % b, tab=(tabs2, 0)),
                                      dict(kT=kT[p3], kn=kTn, v=vtok[p3], vn=vn, tab=(tabs2, 2 + 2 * b))]
                            attend(b * 32, 32, blocks, qT, qTn, gT, gTn, cs)

                def l0_post(t):
                    is_s = (t == NT)
                    par = t % 2
                    p3 = t % 3
                    pv3 = (t - 1) % 3
                    xt = xts[p3]
                    xtn = "xt%d" % p3
                    qT, qTn = qTs[par], "qT%d" % par
                    gT, gTn = gTs[par], "gT%d" % par
                    kTn = "kT%d" % p3
                    vn = "vtok%d" % p3
                    x1 = x1t[par]
                    x1n = "x1t%d" % par
                    ogn = ["ogT%d_%d" % (kvg_, q_) for kvg_ in range(2) for q_ in ((0, 32, 64, 96) if is_s else (0, 64))]
                    for nh in range(2):
                        bk = 2 if nh == 0 else 5
                        for pr in range(8):
                            S.op("pe", ogn + ["wo"], ["ps%d" % bk], lambda nh=nh, pr=pr, bk=bk: nc.tensor.matmul(
                                ps[:, bk, :], lhsT=ogT[:, pr, :], rhs=wo[:, pr, nh * 512:(nh + 1) * 512],
                                start=(pr == 0), stop=(pr == 7)), inc=(pr == 7))
                        S.op("dve", ["ps%d" % bk, xtn], [x1n], lambda nh=nh, x1=x1, xt=xt, bk=bk: nc.vector.tensor_tensor(
                            out=x1[:, nh * 512:(nh + 1) * 512], in0=ps[:, bk, :], in1=xt[:, nh * 512:(nh + 1) * 512], op=ALU.add))
                    S.op("sp", [x1n], ["x1d_%d" % t], lambda x1=x1, t=t: nc.sync.dma_start(
                        out=x1d[t * 128:(t + 1) * 128, :], in_=x1[:]), dsem="st" + x1n)

                def rec_ops(fn_, *a_):
                    S.rec = []
                    fn_(*a_)
                    ops_ = S.rec
                    S.rec = None
                    return ops_

                def to_steps0(ops_):
                    steps_ = []
                    for o_ in ops_:
                        if steps_ and steps_[-1][-1][0] == o_[0] and o_[0] == "pe" and len(steps_[-1]) < PESTEP:
                            steps_[-1].append(o_)
                        else:
                            steps_.append([o_])
                    return steps_

                def emit0(ops_):
                    for (eng_, r_, w_, fn_, inc_, ds_) in ops_:
                        S.op(eng_, r_, w_, fn_, inc=inc_, dsem=ds_)

                l0_load(0)
                l0_load(1)
                PL0 = Planner()

                def emit0p(ops_):
                    for st_ in to_steps0(ops_):
                        PL0.commit(st_)
                        emit0(st_)

                emit0p(rec_ops(l0_front, 0))
                for t in range(NT + 1):
                    S.rec = []
                    l0_load(t + 2)
                    ld_ = S.rec
                    S.rec = None
                    emit0p(ld_)
                    emit0p(rec_ops(l0_pre, t))
                    stC0_ = to_steps0(rec_ops(l0_chunk, t, 0))
                    stC1_ = to_steps0(rec_ops(l0_chunk, t, 1))
                    stZ_ = [[("gate", 1, len(stC0_))], [("gate", 2, len(stC1_))]] + to_steps0(rec_ops(l0_post, t))
                    stF_ = to_steps0(rec_ops(l0_front, t + 1)) if t < NT else []
                    PL0.merge([stF_, stC0_, stC1_, stZ_], emit0)
            S.barrier()

        if layers >= 2:
          with ExitStack() as e1:
            wqkv = sb(e1, "wqkv", [128, 8, 3072], BF16)
            wz = sb(e1, "wz", [128, 8, 1024], BF16)
            wba = sb(e1, "wba", [128, 8, 16], BF16)
            wo1 = sb(e1, "wo1", [128, 8, 1024], BF16)
            diag = sb(e1, "diag", [128, 96, 128], BF16)
            cws = sb(e1, "cws", [128, 96], F32)
            masks = sb(e1, "masks", [128, 7, 128], F32)
            fgt = sb(e1, "fgt", [128, D], F32)
            negA = sb(e1, "negA", [128, 8], F32)
            dtb = sb(e1, "dtb", [128, 8], F32)
            poshalf = sb(e1, "poshalf", [128, 16], F32)
            onesf = sb(e1, "onesf", [128, 128], F32)
            S.op("sp", [], ["cws"], lambda: nc.sync.dma_start(out=cws[:], in_=cwl), dsem="c")
            S.op("sp", [], ["masks"], lambda: nc.sync.dma_start(out=masks[:], in_=masksd), dsem="c")
            S.op("sp", [], ["fgt"], lambda: nc.sync.dma_start(out=fgt[:], in_=fgb), dsem="c")
            S.op("sp", [], ["negA"], lambda: nc.sync.dma_start(out=negA[:], in_=alogb), dsem="c")
            S.op("sp", [], ["dtb"], lambda: nc.sync.dma_start(out=dtb[:], in_=dtbb), dsem="c")
            S.op("act", ["negA"], ["negA"], lambda: nc.scalar.activation(out=negA[:], in_=negA[:], func=AF.Exp))
            S.op("dve", ["negA"], ["negA"], lambda: nc.vector.tensor_scalar(
                out=negA[:], in0=negA[:], scalar1=-1.0, scalar2=None, op0=ALU.mult))
            S.op("dve", [], ["poshalf"], lambda: nc.vector.memset(poshalf[:], 0.5))
            S.op("dve", [], ["onesf"], lambda: nc.vector.memset(onesf[:], 1.0))
            S.op("dve", ["cws", "ident"], ["diag"], lambda: nc.vector.tensor_tensor(
                out=diag[:], in0=ident[:, :].unsqueeze(1).broadcast_to([128, 96, 128]),
                in1=cws[:, :].unsqueeze(2).broadcast_to([128, 96, 128]), op=ALU.mult))

            with ExitStack() as ew:
                stg = [sb(ew, "stg0b", [128, 8, 512], F32), sb(ew, "stg1b", [128, 8, 512], F32)]

                def dst1(c0, c1, st, nm, eng):
                    if c0 < 3072:
                        scaled_cast(eng, wqkv[:, :, c0:c1], st, nm, 512, 1, "wqkv")
                    elif c0 < 4096:
                        scaled_cast(eng, wz[:, :, c0 - 3072:c1 - 3072], st, nm, 512, 1, "wz")
                    else:
                        scaled_cast(eng, wba[:, :, :], st, nm, 16, 1, "wba")

                load_weight(stg, w_in1, 4112, dst1, 1, "wi1")
                load_weight(stg, w_out1, 1024,
                            lambda c0, c1, st, nm, eng: scaled_cast(eng, wo1[:, :, c0:c1], st, nm, 512, 2, "wo1"),
                            None, "wo1")
                S.barrier()

            xts = [sb(e1, "yt0", [128, D], F32), sb(e1, "yt1", [128, D], F32)]
            yo0_ = sb(e1, "yo0", [128, D], F32)
            yo = [yo0_, yo0_]
            ss = sb(e1, "ssb", [128, 1], F32)
            ms = sb(e1, "msb", [128, 1], F32)
            rstd = sb(e1, "rstdb", [128, 1], F32)
            xn = sb(e1, "xnb", [128, D], BF16)
            xnT = sb(e1, "xnTb", [128, 8, 128], BF16)
            junk = xn
            rawb1 = sb(e1, "rawb", [128, 24, 131], BF16)
            halo = sb(e1, "halo", [128, 24, 3], BF16)
            histfs = [sb(e1, "histf%d" % i, [128, 24, 3], F32) for i in range(2)]
            scf = sb(e1, "scf", [128, 24, 4, 3], F32)
            qkvcs = [sb(e1, "qkvc%d" % i, [128, 24, 128], BF16) for i in range(2)]
            sq = sb(e1, "sq", [128, 16, 128], BF16)
            zss = [sb(e1, "zs0", [128, D], BF16), sb(e1, "zs1", [128, D], BF16)]
            kg = sb(e1, "kg", [128, 8, 128], BF16)
            kdec = sb(e1, "kdec", [128, 8, 128], BF16)
            vt = sb(e1, "vt", [128, 8, 128], BF16)
            scs = [sb(e1, "sc0", [128, 20, 8], F32), sb(e1, "sc1", [128, 20, 8], F32)]
            ss16s = [sb(e1, "ss16_%d" % i, [128, 16], F32) for i in range(2)]
            r16 = sb(e1, "r16", [128, 16], F32)
            Ef = sb(e1, "Ef", [128, 8, 128], F32)
            tA = sb(e1, "tA", [128, 8, 128], F32)
            Pb = [sb(e1, "Pb0", [128, 8, 128], BF16), sb(e1, "Pb1", [128, 8, 128], BF16)]
            Qb = [sb(e1, "Qb0", [128, 8, 128], BF16), sb(e1, "Qb1", [128, 8, 128], BF16)]
            Xf = sb(e1, "Xf", [128, 8, 128], F32)
            Xb = sb(e1, "Xb", [128, 8, 128], BF16)
            intraT = sb(e1, "intraT", [128, 8, 128], BF16)
            Sf = sb(e1, "Sf", [128, 8, 128], F32)
            Sbf = sb(e1, "Sbf", [128, 8, 128], BF16)
            wT = sb(e1, "wT", [128, 8, 128], BF16)
            vnew = sb(e1, "vnewb", [128, 8, 128], BF16)
            of = sb(e1, "of", [128, 8, 128], F32)
            ogb = sb(e1, "ogb", [128, D], BF16)
            ogT = sb(e1, "ogTb", [128, 8, 128], BF16)
            sso = sb(e1, "sso", [128, 8], F32)
            S.op("sp", [], ["scf"], lambda: nc.sync.dma_start(out=scf[:], in_=sconv), dsem="c")

            (I_B, I_A, I_E1, I_BETA, I_T, I_G, I_GC, I_GL, I_EGC, I_EDEC, I_EGL, I_DSC, I_AJ, I_CKDEC,
             I_CVT, I_C5, I_C4, I_RK, I_RQ, I_TMP) = range(20)

            def scv(i):
                return sc[:, i, :]

            def bc(ap8):
                return ap8.unsqueeze(2).broadcast_to([128, 8, 128])

            def mbc(mi):
                return masks[:, mi, :].unsqueeze(1).broadcast_to([128, 8, 128])

            def hb(bank0, h):
                return ps[:, bank0 + h // 4, (h % 4) * 128:(h % 4 + 1) * 128]

            def ps2b(bank0):
                return ps[:, bank0:bank0 + 2, :].rearrange("p b (h d) -> p (b h) d", d=128)

            tiles = [("p", t) for t in range(NT)] + [("s", b) for b in range(4)]

            def prefix_ops(ti):
                kind, idx = tiles[ti]
                par = ti % 2
                pz = "p%d" % par
                sc = scs[par]
                zs = zss[par]
                is_s = (kind == "s")
                nv = 32 if is_s else 128
                first = is_s or idx == 0
                last = is_s or idx == NT - 1
                ops = []

                def Sop(eng, reads, writes, fn, inc=True, dsem=None):
                    ops.append((eng, reads, writes, fn, inc, dsem))

                def scv(i):
                    return sc[:, i, :]

                xt = xts[par]
                xtn = "yt%d" % par
                if is_s:
                    Sop("dve", [], [xtn], lambda xt=xt: nc.vector.memset(xt[:], 0.0))
                    r0 = NT * 128 + idx * 32
                    Sop("sp", ["x1d_%d" % NT], [xtn], lambda xt=xt, r0=r0: nc.sync.dma_start(out=xt[0:32, :], in_=x1d[r0:r0 + 32, :]), dsem="ld" + xtn)
                else:
                    Sop("sp", ["x1d_%d" % idx], [xtn], lambda xt=xt, idx=idx: nc.sync.dma_start(out=xt[:], in_=x1d[idx * 128:(idx + 1) * 128, :]), dsem="ld" + xtn)
                    if idx == 0:
                        Sop("dve", [], ["Sf_0", "Sf_1"], lambda: nc.vector.memset(Sf[:], 0.0))
                        Sop("dve", [], ["Sbf_0", "Sbf_1"], lambda: nc.vector.memset(Sbf[:], 0.0))
                rb = rawb1
                if is_s:
                    Sop("pool", ["scf"], ["rawbhalo"], lambda rb=rb, idx=idx: nc.gpsimd.tensor_copy(out=rb[:, :, 0:3], in_=scf[:, :, idx, :]))
                elif idx == 0:
                    Sop("pool", [], ["rawbhalo"], lambda rb=rb: nc.gpsimd.memset(rb[:, :, 0:3], 0.0))
                else:
                    Sop("pool", ["halo_0", "halo_1"], ["rawbhalo"], lambda rb=rb: nc.gpsimd.tensor_copy(out=rb[:, :, 0:3], in_=halo[:]))

                rms_and_transpose(xt, xtn, "ss", xn, xnT, (junk, ss, ms, rstd), jn="xn", emit=Sop)
                for k in range(8):
                    Sop("pe", ["xnT", "wba"], ["ps7"], lambda k=k: nc.tensor.matmul(
                        ps[:, 7, 0:16], lhsT=xnT[:, k, :], rhs=wba[:, k, :], start=(k == 0), stop=(k == 7)), inc=(k == 7))
                Sop("dve", ["ps7"], ["sc_ba" + pz], lambda: nc.vector.tensor_copy(out=sc[:, I_B:I_B + 2, :].rearrange("p a h -> p (a h)"), in_=ps[:, 7, 0:16]))
                Sop("act", ["sc_ba" + pz], ["sc_e1" + pz], lambda: nc.scalar.activation(out=scv(I_E1), in_=scv(I_B), func=AF.Exp, scale=-1.0))
                Sop("dve", ["sc_e1" + pz], ["sc_e1" + pz], lambda: nc.vector.tensor_scalar(out=scv(I_E1), in0=scv(I_E1), scalar1=1.0, scalar2=None, op0=ALU.add))
                Sop("dve", ["sc_e1" + pz], ["sc_beta" + pz], lambda: nc.vector.reciprocal(out=scv(I_BETA), in_=scv(I_E1)))
                Sop("dve", ["sc_ba" + pz, "dtb"], ["sc_t" + pz], lambda: nc.vector.tensor_tensor(out=scv(I_T), in0=scv(I_A), in1=dtb[:], op=ALU.add))
                Sop("act", ["sc_t" + pz], ["sc_t" + pz], lambda: nc.scalar.activation(out=scv(I_T), in_=scv(I_T), func=AF.Exp))
                Sop("dve", ["sc_t" + pz], ["sc_t" + pz], lambda: nc.vector.tensor_scalar(out=scv(I_T), in0=scv(I_T), scalar1=1.0, scalar2=None, op0=ALU.add))
                Sop("act", ["sc_t" + pz], ["sc_t" + pz], lambda: nc.scalar.activation(out=scv(I_T), in_=scv(I_T), func=AF.Ln))
                Sop("dve", ["sc_t" + pz, "negA"], ["sc_g" + pz], lambda: nc.vector.tensor_tensor(out=scv(I_G), in0=scv(I_T), in1=negA[:], op=ALU.mult))
                if is_s:
                    Sop("dve", ["sc_g" + pz, "masks"], ["sc_g" + pz], lambda: nc.vector.tensor_scalar(
                        out=scv(I_G), in0=scv(I_G), scalar1=masks[:, 4, 0:1], scalar2=None, op0=ALU.mult))
                Sop("pe", ["sc_g" + pz, "masks"], ["ps7"], lambda: nc.tensor.matmul(
                    ps[:, 7, 16:24], lhsT=masks[:, 0, :], rhs=scv(I_G), start=True, stop=True), inc=False)
                Sop("pe", ["sc_g" + pz, "onesf"], ["ps7"], lambda: nc.tensor.matmul(
                    ps[:, 7, 24:32], lhsT=onesf[:], rhs=scv(I_G), start=True, stop=True), inc=True)
                Sop("dve", ["ps7"], ["sc_gc" + pz], lambda: nc.vector.tensor_copy(out=sc[:, I_GC:I_GC + 2, :].rearrange("p a h -> p (a h)"), in_=ps[:, 7, 16:32]))
                Sop("act", ["sc_gc" + pz], ["sc_egc" + pz], lambda: nc.scalar.activation(out=scv(I_EGC), in_=scv(I_GC), func=AF.Exp))
                Sop("act", ["sc_gc" + pz], ["sc_egl" + pz], lambda: nc.scalar.activation(out=scv(I_EGL), in_=scv(I_GL), func=AF.Exp))
                Sop("dve", ["sc_gc" + pz], ["sc_edec" + pz], lambda: nc.vector.tensor_tensor(out=scv(I_EDEC), in0=scv(I_GL), in1=scv(I_GC), op=ALU.subtract))
                Sop("act", ["sc_edec" + pz], ["sc_edec" + pz], lambda: nc.scalar.activation(out=scv(I_EDEC), in_=scv(I_EDEC), func=AF.Exp))
                for nh in range(2):
                    for k in range(8):
                        Sop("pe", ["xnT", "wz"], ["ps7"], lambda nh=nh, k=k: nc.tensor.matmul(
                            ps[:, 7, :], lhsT=xnT[:, k, :], rhs=wz[:, k, nh * 512:(nh + 1) * 512], start=(k == 0), stop=(k == 7)), inc=(k == 7))
                    Sop("act", ["ps7"], ["zs" + pz], lambda nh=nh: nc.scalar.activation(
                        out=zs[:, nh * 512:(nh + 1) * 512], in_=ps[:, 7, :], func=AF.Silu))
                return ops

            def gstream(ti, g, front=False):
                kind, idx = tiles[ti]
                par = ti % 2
                pz = "p%d" % par
                sc = scs[par]
                zs = zss[par]
                qkvc = qkvcs[par]
                ss16 = ss16s[par]
                histf = histfs[par]
                is_s = (kind == "s")
                nv = 32 if is_s else 128
                last = is_s or idx == NT - 1
                rb = rawb1
                if True:
                    sfx = "_%d" % g
                    ops = []
                    proj_end = [0]

                    def Sop(eng, reads, writes, fn, inc=True, dsem=None):
                        ops.append((eng, reads, writes, fn, inc, dsem))

                    def scv(i):
                        return sc[:, i, :]
                    pbk, cbk = (5, 6) if front else (1 + g, 3 + g)
                    tbk = 0 if g == 0 else 7
                    PB, CB = "ps%d" % pbk, "ps%d" % cbk
                    h0 = 4 * g

                    def G4(t):
                        return t[:, h0:h0 + 4, :]

                    def sg(i):
                        return sc[:, i, h0:h0 + 4]

                    def bc4(ap4):
                        return ap4.unsqueeze(2).broadcast_to([128, 4, 128])

                    def mbc4(mi):
                        return masks[:, mi, :].unsqueeze(1).broadcast_to([128, 4, 128])

                    def hbk(bank, hi):
                        return ps[:, bank, hi * 128:(hi + 1) * 128]

                    def p4(bank):
                        return ps[:, bank, :].rearrange("p (h d) -> p h d", d=128)

                    for which in range(3):
                        blk0 = which * 8 + h0
                        rn = "rawb%d%s" % (which, sfx)
                        qn = "qkvc%d%s%s" % (which, pz, sfx)
                        for bi in range(4):
                            blk = blk0 + bi
                            for k in range(8):
                                Sop("pe", ["xnT", "wqkv"], [PB], lambda bi=bi, blk=blk, k=k: nc.tensor.matmul(
                                    ps[:, pbk, bi * 128:(bi + 1) * 128], lhsT=wqkv[:, k, blk * 128:(blk + 1) * 128],
                                    rhs=xnT[:, k, :], start=(k == 0), stop=(k == 7)), inc=(k == 7 and bi == 3))
                        Sop("act", [PB], [rn], lambda blk0=blk0: nc.scalar.activation(
                            out=rb[:, blk0:blk0 + 4, 3:131], in_=ps[:, pbk, :].rearrange("p (a t) -> p a t", a=4), func=AF.Copy))
                        if last:
                            Sop("dve", [PB], ["histf" + pz + sfx], lambda blk0=blk0: nc.vector.tensor_copy(
                                out=histf[:, blk0:blk0 + 4, :], in_=ps[:, pbk, :].rearrange("p (a t) -> p a t", a=4)[:, :, nv - 3:nv]))
                        for bi in range(4):
                            blk = blk0 + bi
                            for j in range(4):
                                Sop("pe", ["rawbhalo", rn, "diag"], [CB], lambda bi=bi, blk=blk, j=j: nc.tensor.matmul(
                                    ps[:, cbk, bi * 128:(bi + 1) * 128], lhsT=diag[:, blk * 4 + j, :], rhs=rb[:, blk, j:j + 128],
                                    start=(j == 0), stop=(j == 3)), inc=(j == 3 and bi == 3))
                        Sop("act", [CB], [qn], lambda blk0=blk0: nc.scalar.activation(
                            out=qkvc[:, blk0:blk0 + 4, :].rearrange("p a t -> p (a t)"), in_=ps[:, cbk, :], func=AF.Silu))
                        if which < 2:
                            Sop("act", [qn], ["sq%d%s" % (which, sfx)], lambda blk0=blk0: nc.scalar.activation(
                                out=sq[:, blk0:blk0 + 4, :], in_=qkvc[:, blk0:blk0 + 4, :], func=AF.Square))
                    if not last:
                        Sop("pool", ["rawb%d%s" % (w_, sfx) for w_ in range(3)], ["halo" + sfx], lambda: nc.gpsimd.tensor_copy(
                            out=halo[:].rearrange("p (w g b) t -> p w g b t", w=3, g=2)[:, :, g, :, :],
                            in_=rb[:].rearrange("p (w g b) t -> p w g b t", w=3, g=2)[:, :, g, :, 128:131]))
                    QN, KN, VN = "qkvc0" + pz + sfx, "qkvc1" + pz + sfx, "qkvc2" + pz + sfx
                    for which in range(2):
                        for bi in range(4):
                            blk = which * 8 + h0 + bi
                            Sop("pe", ["sq%d%s" % (which, sfx), "onesb"], [CB], lambda blk=blk, which=which, bi=bi: nc.tensor.matmul(
                                ps[:, cbk, which * 4 + bi:which * 4 + bi + 1], lhsT=sq[:, blk, :], rhs=onesb[:, 0:1], start=True, stop=True),
                                inc=(which == 1 and bi == 3))
                    ss16v = ss16[:].rearrange("p (w h) -> p w h", w=2)[:, :, h0:h0 + 4]
                    r16v = r16[:].rearrange("p (w h) -> p w h", w=2)[:, :, h0:h0 + 4]
                    Sop("dve", [CB], ["ss16" + pz + sfx], lambda: nc.vector.tensor_scalar(
                        out=ss16v, in0=ps[:, cbk, 0:8].rearrange("p (w h) -> p w h", w=2),
                        scalar1=EPS, scalar2=None, op0=ALU.add))
                    proj_end[0] = len(ops)
                    if is_s:
                        Sop("sp", [], ["Sf" + sfx], lambda: nc.sync.dma_start(
                            out=Sf[:, h0:h0 + 4, :], in_=sssm[idx, h0:h0 + 4].rearrange("h k v -> k h v")), dsem="sld%d" % g)
                        Sop("act", ["Sf" + sfx], ["Sbf" + sfx], lambda: nc.scalar.activation(out=G4(Sbf), in_=G4(Sf), func=AF.Copy))
                    Sop("pool", ["ss16" + pz + sfx, "neghalf"], ["r16" + sfx], lambda: nc.gpsimd.tensor_tensor(
                        out=r16v, in0=ss16v, in1=neghalf[:, 0:8].rearrange("p (w h) -> p w h", w=2), op=ALU.pow))
                    rk = r16[:, 8 + h0:8 + h0 + 4]
                    rq = r16[:, h0:h0 + 4]
                    Sop("pool", ["ss16" + pz + sfx, "poshalf"], ["sc_cvt" + pz + sfx], lambda: nc.gpsimd.tensor_tensor(
                        out=sg(I_CVT), in0=ss16[:, 8 + h0:8 + h0 + 4], in1=poshalf[:, 0:4], op=ALU.pow))
                    Sop("dve", ["r16" + sfx, "sc_beta" + pz], ["sc_dsc" + pz + sfx], lambda: nc.vector.tensor_tensor(out=sg(I_DSC), in0=sg(I_BETA), in1=rk, op=ALU.mult))
                    Sop("dve", ["r16" + sfx, "sc_dsc" + pz + sfx], ["sc_aj" + pz + sfx], lambda: nc.vector.tensor_tensor(out=sg(I_AJ), in0=sg(I_DSC), in1=rk, op=ALU.mult))
                    Sop("dve", ["r16" + sfx, "sc_edec" + pz], ["sc_ckdec" + pz + sfx], lambda: nc.vector.tensor_tensor(out=sg(I_CKDEC), in0=sg(I_EDEC), in1=rk, op=ALU.mult))
                    if is_s:
                        Sop("dve", ["sc_ckdec" + pz + sfx, "masks"], ["sc_ckdec" + pz + sfx], lambda: nc.vector.tensor_scalar(
                            out=sg(I_CKDEC), in0=sg(I_CKDEC), scalar1=masks[:, 4, 0:1], scalar2=None, op0=ALU.mult))
                    Sop("dve", ["r16" + sfx], ["sc_c5" + pz + sfx], lambda: nc.vector.tensor_scalar(
                        out=sg(I_C5), in0=rq, scalar1=float(128 ** -0.5), scalar2=None, op0=ALU.mult))
                    Sop("dve", ["sc_c5" + pz + sfx, "sc_egc" + pz], ["sc_c4" + pz + sfx], lambda: nc.vector.tensor_tensor(out=sg(I_C4), in0=sg(I_C5), in1=sg(I_EGC), op=ALU.mult))
                    pTk = psb(pbk)
                    for hi in range(4):
                        Sop("pe", [KN, "identb"], [PB], lambda hi=hi: nc.tensor.transpose(
                            out=pTk[:, hi * 128:(hi + 1) * 128], in_=qkvc[:, 8 + h0 + hi, :], identity=identb[:]), inc=(hi == 3))
                    pTkv = pTk[:, 0:512].rearrange("p (h d) -> p h d", h=4)
                    Sop("dve", [PB, "sc_egc" + pz], ["kg" + sfx], lambda: nc.vector.tensor_tensor(out=G4(kg), in0=pTkv, in1=bc4(sg(I_EGC)), op=ALU.mult))
                    Sop("dve", [PB, "sc_ckdec" + pz + sfx], ["kdec" + sfx], lambda: nc.vector.tensor_tensor(out=G4(kdec), in0=pTkv, in1=bc4(sg(I_CKDEC)), op=ALU.mult))
                    pTv = psb(cbk)
                    for hi in range(4):
                        Sop("pe", [VN, "identb"], [CB], lambda hi=hi: nc.tensor.transpose(
                            out=pTv[:, hi * 128:(hi + 1) * 128], in_=qkvc[:, 16 + h0 + hi, :], identity=identb[:]), inc=(hi == 3))
                    Sop("dve", [CB, "sc_cvt" + pz + sfx], ["vt" + sfx], lambda: nc.vector.tensor_tensor(
                        out=G4(vt), in0=pTv[:, 0:512].rearrange("p (h d) -> p h d", h=4), in1=bc4(sg(I_CVT)), op=ALU.mult))
                    Sop("dve", ["sc_g" + pz, "masks"], ["Xf" + sfx], lambda: nc.vector.tensor_tensor(out=G4(Xf), in0=mbc4(1), in1=bc4(sg(I_G)), op=ALU.mult))
                    for hi in range(4):
                        Sop("pe", ["Xf" + sfx, "masks"], [PB], lambda hi=hi: nc.tensor.matmul(
                            hbk(pbk, hi), lhsT=Xf[:, h0 + hi, :], rhs=masks[:, 0, :], start=True, stop=True), inc=(hi == 3))
                    Sop("act", [PB], ["Ef" + sfx], lambda: nc.scalar.activation(out=G4(Ef), in_=p4(pbk), func=AF.Exp))
                    for hi in range(4):
                        Sop("pe", [KN], [PB], lambda hi=hi: nc.tensor.matmul(
                            hbk(pbk, hi), lhsT=qkvc[:, 8 + h0 + hi, :], rhs=qkvc[:, 8 + h0 + hi, :], start=True, stop=True), inc=(hi == 3))
                    for hi in range(4):
                        Sop("pe", [KN, QN], [CB], lambda hi=hi: nc.tensor.matmul(
                            hbk(cbk, hi), lhsT=qkvc[:, 8 + h0 + hi, :], rhs=qkvc[:, h0 + hi, :], start=True, stop=True), inc=(hi == 3))
                    Sop("pool", ["Ef" + sfx, "masks"], ["tA" + sfx], lambda: nc.gpsimd.tensor_tensor(out=G4(tA), in0=G4(Ef), in1=mbc4(2), op=ALU.mult))
                    Sop("pool", ["tA" + sfx, "sc_aj" + pz + sfx], ["tA" + sfx], lambda: nc.gpsimd.tensor_tensor(out=G4(tA), in0=G4(tA), in1=bc4(sg(I_AJ)), op=ALU.mult))
                    Sop("dve", [PB, "tA" + sfx], ["Xf" + sfx], lambda: nc.vector.tensor_tensor(out=G4(Xf), in0=p4(pbk), in1=G4(tA), op=ALU.mult))
                    Sop("act", ["Xf" + sfx], ["wT" + sfx], lambda: nc.scalar.activation(out=G4(wT), in_=G4(Xf), func=AF.Copy))
                    Sop("dve", ["Xf" + sfx, "masks"], ["Xf" + sfx], lambda: nc.vector.tensor_tensor(out=G4(Xf), in0=G4(Xf), in1=mbc4(4), op=ALU.mult))
                    Sop("act", ["Xf" + sfx], ["Pb0" + sfx], lambda: nc.scalar.activation(out=G4(Pb[0]), in_=G4(Xf), func=AF.Copy))
                    Sop("dve", ["Xf" + sfx, "ident"], ["Xf" + sfx], lambda: nc.vector.tensor_tensor(
                        out=G4(Xf), in0=ident[:, :].unsqueeze(1).broadcast_to([128, 4, 128]), in1=G4(Xf), op=ALU.subtract))
                    Sop("act", ["Xf" + sfx], ["Xb" + sfx], lambda: nc.scalar.activation(out=G4(Xb), in_=G4(Xf), func=AF.Copy))
                    Sop("pool", ["Ef" + sfx, "masks"], ["Ef" + sfx], lambda: nc.gpsimd.tensor_tensor(out=G4(Ef), in0=G4(Ef), in1=mbc4(3), op=ALU.mult))
                    Sop("pool", ["Ef" + sfx, "r16" + sfx], ["Ef" + sfx], lambda: nc.gpsimd.tensor_tensor(out=G4(Ef), in0=G4(Ef), in1=bc4(rk), op=ALU.mult))
                    Sop("dve", [CB, "Ef" + sfx], ["intraT" + sfx], lambda: nc.vector.tensor_tensor(out=G4(intraT), in0=p4(cbk), in1=G4(Ef), op=ALU.mult))
                    pTq = psb(cbk)
                    for hi in range(4):
                        Sop("pe", ["wT" + sfx, "identb"], [CB], lambda hi=hi: nc.tensor.transpose(
                            out=pTq[:, hi * 128:(hi + 1) * 128], in_=wT[:, h0 + hi, :], identity=identb[:]), inc=(hi == 3))
                    pTq4 = pTq[:, 0:512].rearrange("p (h d) -> p h d", h=4)
                    Sop("dve", [CB, "masks"], ["Qb0" + sfx], lambda: nc.vector.tensor_tensor(out=G4(Qb[0]), in0=pTq4, in1=mbc4(4), op=ALU.mult))
                    Sop("act", [CB], ["wT" + sfx], lambda: nc.scalar.activation(out=G4(wT), in_=pTq4, func=AF.Copy))
                    Sop("pool", ["wT" + sfx, "masks"], ["vnew" + sfx], lambda: nc.gpsimd.tensor_tensor(out=G4(vnew), in0=G4(wT), in1=mbc4(5), op=ALU.mult))
                    NLEV = 4
                    for s_ in range(1, NLEV + 1):
                        Pp, Qp = Pb[(s_ - 1) % 2], Qb[(s_ - 1) % 2]
                        Pn, Qn = Pb[s_ % 2], Qb[s_ % 2]
                        Ppn, Qpn = "Pb%d%s" % ((s_ - 1) % 2, sfx), "Qb%d%s" % ((s_ - 1) % 2, sfx)
                        Pnn, Qnn = "Pb%d%s" % (s_ % 2, sfx), "Qb%d%s" % (s_ % 2, sfx)
                        for hi in range(4):
                            Sop("pe", [Ppn, Qpn], [CB], lambda hi=hi, Pp=Pp, Qp=Qp: nc.tensor.matmul(
                                hbk(cbk, hi), lhsT=Pp[:, h0 + hi, :], rhs=Qp[:, h0 + hi, :], start=True, stop=True), inc=(hi == 3))
                        if s_ < NLEV:
                            for hi in range(4):
                                Sop("pe", [Ppn, Qpn], [PB], lambda hi=hi, Pp=Pp, Qp=Qp: nc.tensor.matmul(
                                    hbk(pbk, hi), lhsT=Qp[:, h0 + hi, :], rhs=Pp[:, h0 + hi, :], start=True, stop=True), inc=(hi == 3))
                        Sop("act", [CB], [Qnn], lambda Qn=Qn: nc.scalar.activation(out=G4(Qn), in_=p4(cbk), func=AF.Copy))
                        if s_ < NLEV:
                            Sop("dve", [PB], [Pnn], lambda Pn=Pn: nc.vector.tensor_copy(out=G4(Pn), in_=p4(pbk)))
                        for hi in range(4):
                            Sop("pe", [Qnn, "Xb" + sfx], [PB], lambda hi=hi, Qn=Qn: nc.tensor.matmul(
                                hbk(pbk, hi), lhsT=Qn[:, h0 + hi, :], rhs=Xb[:, h0 + hi, :], start=True, stop=True), inc=(hi == 3))
                        Sop("dve", [PB, "Xf" + sfx], ["Xf" + sfx], lambda: nc.vector.tensor_tensor(out=G4(Xf), in0=p4(pbk), in1=G4(Xf), op=ALU.add))
                        Sop("act", ["Xf" + sfx], ["Xb" + sfx], lambda: nc.scalar.activation(out=G4(Xb), in_=G4(Xf), func=AF.Copy))
                    for mi in (5, 6):
                        pTx = psb(cbk)
                        for hi in range(4):
                            Sop("pe", ["Xb" + sfx, "identb"], [CB], lambda hi=hi, pTx=pTx: nc.tensor.transpose(
                                out=pTx[:, hi * 128:(hi + 1) * 128], in_=Xb[:, h0 + hi, :], identity=identb[:]), inc=(hi == 3))
                        Sop("act", [CB], ["Pb1" + sfx], lambda pTx=pTx: nc.scalar.activation(
                            out=G4(Pb[1]), in_=pTx[:, 0:512].rearrange("p (h d) -> p h d", h=4), func=AF.Copy))
                        for hi in range(4):
                            Sop("pe", ["vnew" + sfx, "Xb" + sfx], [PB], lambda hi=hi: nc.tensor.matmul(
                                hbk(pbk, hi), lhsT=vnew[:, h0 + hi, :], rhs=Xb[:, h0 + hi, :], start=True, stop=True), inc=(hi == 3))
                        Sop("dve", [PB], ["Pb0" + sfx], lambda: nc.vector.tensor_copy(out=G4(Pb[0]), in_=p4(pbk)))
                        if mi == 5:
                            Sop("pool", ["wT" + sfx, "masks"], ["vnew" + sfx], lambda: nc.gpsimd.tensor_tensor(out=G4(vnew), in0=G4(wT), in1=mbc4(6), op=ALU.mult))
                        for hi in range(4):
                            Sop("pe", ["Pb1" + sfx, "Pb0" + sfx], [PB], lambda hi=hi: nc.tensor.matmul(
                                hbk(pbk, hi), lhsT=Pb[1][:, h0 + hi, :], rhs=Pb[0][:, h0 + hi, :], start=True, stop=True), inc=(hi == 3))
                        Sop("dve", [PB, "Xf" + sfx], ["Xf" + sfx], lambda: nc.vector.tensor_tensor(out=G4(Xf), in0=G4(Xf), in1=p4(pbk), op=ALU.subtract))
                        Sop("act", ["Xf" + sfx], ["Xb" + sfx], lambda: nc.scalar.activation(out=G4(Xb), in_=G4(Xf), func=AF.Copy))
                    for hi in range(4):
                        Sop("pe", ["kg" + sfx, "Xb" + sfx], [PB], lambda hi=hi: nc.tensor.matmul(
                            hbk(pbk, hi), lhsT=kg[:, h0 + hi, :], rhs=Xb[:, h0 + hi, :], start=True, stop=True), inc=(hi == 3))
                    Sop("act", [PB], ["wT" + sfx], lambda: nc.scalar.activation(out=G4(wT), in_=p4(pbk), func=AF.Copy, scale=-1.0))
                    for hi in range(4):
                        Sop("pe", ["Xb" + sfx, "vt" + sfx], [CB], lambda hi=hi: nc.tensor.matmul(
                            hbk(cbk, hi), lhsT=Xb[:, h0 + hi, :], rhs=vt[:, h0 + hi, :], start=True, stop=False), inc=False)
                        Sop("pe", ["wT" + sfx, "Sbf" + sfx], [CB], lambda hi=hi: nc.tensor.matmul(
                            hbk(cbk, hi), lhsT=wT[:, h0 + hi, :], rhs=Sbf[:, h0 + hi, :], start=False, stop=True), inc=(hi == 3))
                    Sop("dve", [CB, "sc_dsc" + pz + sfx], ["vnew" + sfx], lambda: nc.vector.tensor_tensor(out=G4(vnew), in0=p4(cbk), in1=bc4(sg(I_DSC)), op=ALU.mult))
                    for hi in range(4):
                        Sop("pe", [QN, "Sbf" + sfx], [PB], lambda hi=hi: nc.tensor.matmul(
                            hbk(pbk, hi), lhsT=qkvc[:, h0 + hi, :], rhs=Sbf[:, h0 + hi, :], start=True, stop=True), inc=(hi == 3))
                    for hi in range(4):
                        Sop("pe", ["intraT" + sfx, "vnew" + sfx], [CB], lambda hi=hi: nc.tensor.matmul(
                            hbk(cbk, hi), lhsT=intraT[:, h0 + hi, :], rhs=vnew[:, h0 + hi, :], start=True, stop=True), inc=(hi == 3))
                    Sop("dve", [PB, "sc_c4" + pz + sfx], ["of" + sfx], lambda: nc.vector.tensor_tensor(out=G4(of), in0=p4(pbk), in1=bc4(sg(I_C4)), op=ALU.mult))
                    Sop("dve", [CB, "sc_c5" + pz + sfx], ["tA" + sfx], lambda: nc.vector.tensor_tensor(out=G4(tA), in0=p4(cbk), in1=bc4(sg(I_C5)), op=ALU.mult))
                    Sop("pool", ["of" + sfx, "tA" + sfx], ["of" + sfx], lambda: nc.gpsimd.tensor_tensor(out=G4(of), in0=G4(of), in1=G4(tA), op=ALU.add))
                    for hi in range(4):
                        Sop("pe", ["kdec" + sfx, "vnew" + sfx], [PB], lambda hi=hi: nc.tensor.matmul(
                            hbk(pbk, hi), lhsT=kdec[:, h0 + hi, :], rhs=vnew[:, h0 + hi, :], start=True, stop=True), inc=(hi == 3))
                    for hi in range(4):
                        Sop("dve", [PB, "Sf" + sfx, "sc_egl" + pz], ["Sf" + sfx], lambda hi=hi: nc.vector.scalar_tensor_tensor(
                            out=Sf[:, h0 + hi, :], in0=Sf[:, h0 + hi, :], scalar=sc[:, I_EGL, h0 + hi:h0 + hi + 1], in1=hbk(pbk, hi),
                            op0=ALU.mult, op1=ALU.add))
                    Sop("act", ["Sf" + sfx], ["Sbf" + sfx], lambda: nc.scalar.activation(out=G4(Sbf), in_=G4(Sf), func=AF.Copy))
                    ssov = sso[:, h0:h0 + 4]
                    Sop("act", ["of" + sfx], ["Ef" + sfx], lambda: nc.scalar.activation(out=G4(Ef), in_=G4(of), func=AF.Square))
                    Sop("dve", ["Ef" + sfx], ["sso" + sfx], lambda: nc.vector.tensor_reduce(out=ssov, in_=G4(Ef), axis=mybir.AxisListType.X, op=ALU.add))
                    Sop("dve", ["sso" + sfx], ["sso" + sfx], lambda: nc.vector.tensor_scalar(
                        out=ssov, in0=ssov, scalar1=1.0 / 128, scalar2=EPS, op0=ALU.mult, op1=ALU.add))
                    Sop("pool", ["sso" + sfx, "neghalf"], ["sso" + sfx], lambda: nc.gpsimd.tensor_tensor(out=ssov, in0=ssov, in1=neghalf[:, 0:4], op=ALU.pow))
                    Sop("dve", ["of" + sfx, "sso" + sfx], ["of" + sfx], lambda: nc.vector.tensor_tensor(out=G4(of), in0=G4(of), in1=bc4(ssov), op=ALU.mult))
                    Sop("dve", ["of" + sfx, "zs" + pz], ["ogb" + sfx], lambda: nc.vector.tensor_tensor(
                        out=ogb[:, h0 * 128:(h0 + 4) * 128], in0=G4(of).rearrange("p h d -> p (h d)"), in1=zs[:, h0 * 128:(h0 + 4) * 128], op=ALU.mult))
                    return ops, proj_end[0]

            def suffix_ops(ti):
                kind, idx = tiles[ti]
                par = ti % 2
                is_s = (kind == "s")
                xt = xts[par]
                xtn = "yt%d" % par
                ops = []

                def Sop(eng, reads, writes, fn, inc=True, dsem=None):
                    ops.append((eng, reads, writes, fn, inc, dsem))

                pT0 = psb(0)
                for k in range(8):
                    Sop("pe", ["ogb_0", "ogb_1", "identb"], ["ps0"], lambda k=k: nc.tensor.transpose(
                        out=pT0[:, k * 128:(k + 1) * 128], in_=ogb[:, k * 128:(k + 1) * 128], identity=identb[:]), inc=(k == 7))
                Sop("act", ["ps0"], ["ogT"], lambda: nc.scalar.activation(
                    out=ogT[:].rearrange("p k t -> p (k t)"), in_=pT0[:, 0:1024], func=AF.Copy))
                for nh in range(2):
                    for k in range(8):
                        Sop("pe", ["ogT", "wo1"], ["ps7"], lambda nh=nh, k=k: nc.tensor.matmul(
                            ps[:, 7, :], lhsT=ogT[:, k, :], rhs=wo1[:, k, nh * 512:(nh + 1) * 512],
                            start=(k == 0), stop=(k == 7)), inc=(k == 7))
                    Sop("dve", ["ps7", xtn], ["yo0"], lambda nh=nh, xt=xt, par=par: nc.vector.tensor_tensor(
                        out=yo[par][:, nh * 512:(nh + 1) * 512], in0=ps[:, 7, :], in1=xt[:, nh * 512:(nh + 1) * 512], op=ALU.add))
                Sop("act", ["yo0"], ["xn", "ss"], lambda par=par: nc.scalar.activation(out=junk[:], in_=yo[par][:], func=AF.Square, accum_out=ss[:, 0:1]))
                Sop("dve", ["ss"], ["ms"], lambda: nc.vector.tensor_scalar(
                    out=ms[:, 0:1], in0=ss[:, 0:1], scalar1=1.0 / D, scalar2=EPS, op0=ALU.mult, op1=ALU.add))
                Sop("pool", ["ms", "neghalf"], ["rstd"], lambda: nc.gpsimd.tensor_tensor(out=rstd[:, 0:1], in0=ms[:, 0:1], in1=neghalf[:, 0:1], op=ALU.pow))
                y = yo[par]
                yn = "yo0"
                Sop("dve", [yn, "rstd", "fgt"], [yn], lambda y=y: nc.vector.scalar_tensor_tensor(
                    out=y[:], in0=y[:], scalar=rstd[:, 0:1], in1=fgt[:], op0=ALU.mult, op1=ALU.mult))
                if is_s:
                    Sop("sp", [yn], ["ys_out"], lambda y=y, idx=idx: nc.sync.dma_start(out=ys[idx * 32:(idx + 1) * 32, :], in_=y[0:32, :]), dsem="st" + yn)
                else:
                    Sop("sp", [yn], ["yp_out"], lambda y=y, idx=idx: nc.sync.dma_start(out=yp[idx * 128:(idx + 1) * 128, :], in_=y[:]), dsem="st" + yn)
                return ops

            def emit_last(ti):
                kind, idx = tiles[ti]
                histf = histfs[ti % 2]
                pz = "p%d" % (ti % 2)
                is_s = (kind == "s")
                last = True
                if last:
                    dst = ssms[idx] if is_s else ssmp
                    S.op("sp", ["Sf_0", "Sf_1"], ["ssm_out"], lambda dst=dst: nc.sync.dma_start(out=dst.rearrange("h k v -> k h v"), in_=Sf[:]), dsem="o")
                    dstc = convs[idx] if is_s else convp
                    hv = Ef[0:3, :, :].rearrange("p h d -> p (h d)")
                    for rd in range(3):
                        for bl in range(8):
                            blk = rd * 8 + bl
                            S.op("pe", ["histf" + pz + "_0", "histf" + pz + "_1", "ident"], ["ps5", "ps6"], lambda blk=blk, bl=bl: nc.tensor.transpose(
                                out=ps[0:3, 5 + bl // 4, (bl % 4) * 128:(bl % 4 + 1) * 128], in_=histf[:, blk, :], identity=ident[:]), inc=(bl == 7))
                        S.op("dve", ["ps5", "ps6"], ["Ef_0", "Ef_1"], lambda hv=hv: nc.vector.tensor_copy(
                            out=hv, in_=ps[0:3, 5:7, :].rearrange("p b c -> p (b c)")))
                        S.op("sp", ["Ef_0", "Ef_1"], ["conv_out"], lambda dstc=dstc, rd=rd, hv=hv: nc.sync.dma_start(out=dstc[:, rd * 1024:(rd + 1) * 1024], in_=hv), dsem="o")

            def to_steps(ops_, maxpe=PESTEP):
                steps_ = []
                for o_ in ops_:
                    if steps_ and steps_[-1][-1][0] == o_[0] and o_[0] == "pe" and len(steps_[-1]) < maxpe:
                        steps_[-1].append(o_)
                    else:
                        steps_.append([o_])
                return steps_

            def emit_ops(ops_):
                for (eng_, r_, w_, fn_, inc_, ds_) in ops_:
                    S.op(eng_, r_, w_, fn_, inc=inc_, dsem=ds_)

            deferred = None
            prefix_done = set()
            PL1 = Planner()

            def emit_ops_p(ops_):
                for st_ in to_steps(ops_):
                    PL1.commit(st_)
                    emit_ops(st_)

            def front_ops(ti_, g_):
                o_, c_ = gstream(ti_, g_, front=True)
                return o_[:c_]

            for ti, (kind, idx) in enumerate(tiles):
                is_s = (kind == "s")
                last = is_s or idx == NT - 1
                if ti not in prefix_done:
                    emit_ops_p(prefix_ops(ti))
                    emit_ops_p(front_ops(ti, 0))
                    emit_ops_p(front_ops(ti, 1))
                opsA, cA = gstream(ti, 0)
                opsB, cB = gstream(ti, 1)
                stA, stB = to_steps(opsA[cA:]), to_steps(opsB[cB:])
                stX = to_steps(deferred) if deferred else []
                stX2 = []
                if ti + 1 < len(tiles) and OVERLAP:
                    opsD = prefix_ops(ti + 1)
                    prefix_done.add(ti + 1)
                    ldD = []
                    while opsD and opsD[0][0] == "sp":
                        ldD.append(opsD.pop(0))
                    stX = stX + ([ldD] if ldD else []) + to_steps(opsD)
                    gate_at = None
                    for si_, st_ in enumerate(stX):
                        if any("xnT" in o_[2] for o_ in st_):
                            gate_at = si_ + 1
                    assert gate_at is not None
                    stX2 = [[("gate", 2, gate_at)]] + to_steps(front_ops(ti + 1, 0)) + to_steps(front_ops(ti + 1, 1))
                if PLAN:
                    PL1.merge([stA, stB, stX, stX2], emit_ops)
                else:
                    for st_ in stA + stB + stX + stX2[1:]:
                        emit_ops(st_)
                if last:
                    emit_ops_p(suffix_ops(ti))
                    emit_last(ti)
                    deferred = None
                else:
                    if OVERLAP:
                        deferred = suffix_ops(ti)
                    else:
                        emit_ops_p(suffix_ops(ti))
                        deferred = None
            if deferred:
                emit_ops(deferred)
            S.barrier()

        S.finish()
    return nc


_CACHE = {}


def _get_program(NT, layers=2, debug=False):
    key = (NT, layers, debug)
    if key not in _CACHE:
        _CACHE[key] = build_program(NT, layers, debug)
    return _CACHE[key]


def make_in_maps(inputs, NT):
    f = lambda a: np.ascontiguousarray(np.asarray(a, dtype=np.float32))
    x_prompt = f(inputs["x_prompt"]); x_sample = f(inputs["x_sample"])
    cache_k = f(inputs["cache_k"]); cache_v = f(inputs["cache_v"])
    state_conv = f(inputs["state_conv"]); state_ssm = f(inputs["state_ssm"])
    norm_g = f(inputs["norm_g"])
    gcol = np.ascontiguousarray(norm_g.reshape(2, 8, 128).transpose(2, 0, 1))
    dngc = np.broadcast_to(f(inputs["dn_norm_g"])[0][:, None, None], (128, 1, 8))
    gcol = np.ascontiguousarray(np.concatenate([gcol, dngc], axis=1))
    fgb = np.ascontiguousarray(np.broadcast_to(f(inputs["final_norm_g"])[None, :], (128, D)))
    sinks = f(inputs["attn_sinks"])[0]
    sr = np.zeros((2, 4, 2, 64), np.float32)
    for half in range(2):
        for kv in range(4):
            for j in range(2):
                sr[half, kv, j, :] = sinks[4 * kv + 2 * j + half]
    sinkrow = np.concatenate([sr.reshape(1, 1024), sr[:, :, :, :32].reshape(1, 512)], axis=1)
    tabs_h, tabs2_h = _alibi_tables()
    cw = f(inputs["dn_conv_w"])[0]
    cwl = np.ascontiguousarray(cw.reshape(4, 24, 128).transpose(2, 1, 0)).reshape(128, 96)
    alogb = np.ascontiguousarray(np.broadcast_to(f(inputs["dn_a_log"])[0][None, :], (128, 8)))
    dtbb = np.ascontiguousarray(np.broadcast_to(f(inputs["dn_dt_bias"])[0][None, :], (128, 8)))
    dngb = np.ascontiguousarray(np.broadcast_to(f(inputs["dn_norm_g"])[0][None, :], (128, 128)))
    common = dict(
        gcol=gcol, fgb=fgb, w_in0=f(inputs["attn_w_in"])[0], w_out0=f(inputs["attn_w_out"])[0],
        w_in1=f(inputs["dn_w_in"])[0], w_out1=f(inputs["dn_w_out"])[0], sinkrow=sinkrow, cwl=cwl,
        alogb=alogb, dtbb=dtbb, dngb=dngb, ident=np.eye(128, dtype=np.float32),
        tabs=tabs_h, tabs2=tabs2_h, masks=_masks_l1())
    maps = []
    for c in range(NCORES):
        sc = state_conv[0, 4 * c:4 * c + 4]
        sconv = np.ascontiguousarray(sc.reshape(4, 3, 24, 128).transpose(3, 2, 0, 1))
        m = dict(common)
        m.update(
            xp=np.ascontiguousarray(x_prompt[c, :NT * 128]),
            xs=np.ascontiguousarray(x_sample[4 * c:4 * c + 4].reshape(128, D)),
            ck=np.ascontiguousarray(cache_k[0, 4 * c:4 * c + 4].reshape(4, 128, 256)),
            cv=np.ascontiguousarray(cache_v[0, 4 * c:4 * c + 4].reshape(4, 128, 256)),
            sconv=sconv, sssm=np.ascontiguousarray(state_ssm[0, 4 * c:4 * c + 4]))
        maps.append(m)
    return maps


def run(inputs, NT=NT_FULL, layers=2, debug=False, trace=False):
    nc = _get_program(NT, layers, debug)
    in_maps = make_in_maps(inputs, NT)
    res = run_bass_kernel_spmd(nc, in_maps, core_ids=list(range(NCORES)))
    return res


def kernel(**inputs):
    NT = NT_FULL
    res = run(inputs, NT)
    R = res.results
    st = lambda name: np.stack([np.asarray(R[c][name]) for c in range(NCORES)])
    y_prompt = st("yp").reshape(NCORES, NT * 128, D)
    y_sample = st("ys").reshape(NCORES * 4, 32, D)
    kwp = st("kwp").reshape(1, NCORES, 128, 4, 64)
    vwp = st("vwp").reshape(1, NCORES, 128, 4, 64)
    convp = st("convp").reshape(1, NCORES, 3, 3072)
    ssmp = st("ssmp").reshape(1, NCORES, 8, 128, 128)
    kws = st("kws").reshape(1, NCORES * 4, 128, 4, 64)
    vws = st("vws").reshape(1, NCORES * 4, 128, 4, 64)
    convs = st("convs").reshape(1, NCORES * 4, 3, 3072)
    ssms = st("ssms").reshape(1, NCORES * 4, 8, 128, 128)
    return (y_prompt, y_sample, kwp, vwp, convp, ssmp, kws, vws, convs, ssms)
```
